# Optimizing a Trainium2 kernel written in Bass

```python
import math
import jax, jax.numpy as jnp
from jax import lax
import numpy as np

D_MODEL = 1024
BATCH = 8
SEQ = 2048
DEPTH = 1
DEC_BATCH = 128
DEC_SEQ = 1
PAST_LEN = 16384
PAGE_SIZE = 128

HEAD_DIM = 64
N_Q_HEADS = D_MODEL // HEAD_DIM
N_KV_HEADS = 2
GROUP = N_Q_HEADS // N_KV_HEADS
ATTN_WIDTH = N_Q_HEADS * HEAD_DIM
KV_WIDTH = N_KV_HEADS * HEAD_DIM
WINDOW = 128
ATTN_BLOCK = WINDOW
N_BUCKETS = 32
MAX_DISTANCE = 128
POOL_WINDOWS = (2, 4, 8, 16)
N_POOL_GROUPS = len(POOL_WINDOWS)
POOL_WIDTH = D_MODEL // 2
POOL_GROUP = POOL_WIDTH // N_POOL_GROUPS
POOL_OUT_GROUP = D_MODEL // N_POOL_GROUPS
POOL_HIST = max(POOL_WINDOWS) - 1
IN_WIDTH = ATTN_WIDTH + 2 * KV_WIDTH + POOL_WIDTH + 2 * D_MODEL
PEER_HEADS = 8
PEER_N_KEYS = 128
PEER_N_EXPERTS = PEER_N_KEYS * PEER_N_KEYS
PEER_KEY_DIM = 256
PEER_HALF = PEER_KEY_DIM // 2
PEER_TOPK = 16
PEER_CHUNK = 256
EPS = 1e-6

kernel_name = "hybrid_swa_pool_peer_adaln_step"


def rms_norm(x, g):
    xf = x.astype(jnp.float32)
    y = xf * lax.rsqrt(jnp.mean(xf * xf, axis=-1, keepdims=True) + EPS)
    return (y * g.astype(jnp.float32)).astype(x.dtype)


def rel_bucket(dist):
    d = jnp.maximum(dist, 0)
    max_exact = N_BUCKETS // 2
    df = jnp.maximum(d, 1).astype(jnp.float32)
    large = max_exact + (jnp.log(df / max_exact) / math.log(MAX_DISTANCE / max_exact)
                         * (N_BUCKETS - max_exact)).astype(jnp.int32)
    large = jnp.minimum(large, N_BUCKETS - 1)
    return jnp.where(d < max_exact, d, large)


def rel_bias(dist, table):
    b = table[rel_bucket(dist)]
    b = jnp.transpose(b, (2, 0, 1)).astype(jnp.float32)
    return b.reshape(N_KV_HEADS, GROUP, dist.shape[0], dist.shape[1])


def sink_attention(q, k, v, bias, mask, sinks):
    s = jnp.einsum('...qhgd,...khd->...hgqk', q, k,
                   preferred_element_type=jnp.float32) * (HEAD_DIM ** -0.5) + bias
    s = jnp.where(mask, s, -jnp.inf)
    sink = sinks.astype(jnp.float32).reshape(N_KV_HEADS, GROUP, 1, 1)
    m = jnp.maximum(jnp.max(s, axis=-1, keepdims=True), sink)
    p = jnp.exp(s - m)
    denom = jnp.sum(p, axis=-1, keepdims=True) + jnp.exp(sink - m)
    w = (p / denom).astype(v.dtype)
    return jnp.einsum('...hgqk,...khd->...qhgd', w, v)


def prompt_attention(q, k, v, table, sinks):
    B, T = q.shape[:2]
    nb = T // ATTN_BLOCK
    qb = q.reshape(B, nb, ATTN_BLOCK, N_KV_HEADS, GROUP, HEAD_DIM)

    def band(t):
        cur = t.reshape(B, nb, ATTN_BLOCK, N_KV_HEADS, HEAD_DIM)
        prev = jnp.concatenate([jnp.zeros_like(cur[:, :1]), cur[:, :-1]], axis=1)
        return jnp.concatenate([prev, cur], axis=2)

    kb, vb = band(k), band(v)
    qi = jnp.arange(ATTN_BLOCK)[:, None]
    kj = jnp.arange(2 * ATTN_BLOCK)[None, :]
    dist = qi + ATTN_BLOCK - kj
    blk = jnp.arange(nb)[:, None, None]
    valid = (dist >= 0) & (dist < WINDOW) & (blk * ATTN_BLOCK - ATTN_BLOCK + kj >= 0)
    mask = valid[None, :, None, None]
    out = sink_attention(qb, kb, vb, rel_bias(dist, table), mask, sinks)
    return out.reshape(B, T, ATTN_WIDTH)


def sample_attention(q, k, v, cache_k, cache_v, table, sinks):
    B, T = q.shape[:2]
    k_all = jnp.concatenate([cache_k, k], axis=1)
    v_all = jnp.concatenate([cache_v, v], axis=1)
    dist = jnp.arange(T)[:, None] + WINDOW - jnp.arange(WINDOW + T)[None, :]
    valid = (dist >= 0) & (dist < WINDOW)
    qg = q.reshape(B, T, N_KV_HEADS, GROUP, HEAD_DIM)
    out = sink_attention(qg, k_all, v_all, rel_bias(dist, table), valid, sinks)
    return out.reshape(B, T, ATTN_WIDTH), k_all[:, -WINDOW:], v_all[:, -WINDOW:]


def pool_mix(p_ext, pos, w_pool, pool_scale):
    T = pos.shape[0]
    pf = p_ext.astype(jnp.float32)
    cs = jnp.concatenate([jnp.zeros_like(pf[:, :1]), jnp.cumsum(pf, axis=1)], axis=1)
    outs = []
    for g, w in enumerate(POOL_WINDOWS):
        sl = slice(g * POOL_GROUP, (g + 1) * POOL_GROUP)
        tot = cs[:, POOL_HIST + 1:, sl] - cs[:, POOL_HIST + 1 - w:POOL_HIST + 1 - w + T, sl]
        cnt = jnp.minimum(w, pos + 1).astype(jnp.float32)[None, :, None]
        r = (tot / cnt - pf[:, POOL_HIST:, sl]).astype(p_ext.dtype)
        outs.append(r @ w_pool[g])
    return jnp.concatenate(outs, axis=-1) * pool_scale


def peer_ffn(h, w_query, sub_keys, u, v):
    B, T, D = h.shape
    n = B * T
    xt = h.reshape(n, D)
    qry = (xt @ w_query).reshape(n, PEER_HEADS, 2, PEER_HALF)
    s = jnp.einsum('nhpd,hpkd->nhpk', qry, sub_keys, preferred_element_type=jnp.float32)
    top_s, top_i = lax.top_k(s, PEER_TOPK)
    cand_s = (top_s[:, :, 0, :, None] + top_s[:, :, 1, None, :]).reshape(n, PEER_HEADS, PEER_TOPK * PEER_TOPK)
    cand_i = (top_i[:, :, 0, :, None] * PEER_N_KEYS + top_i[:, :, 1, None, :]).reshape(n, PEER_HEADS, PEER_TOPK * PEER_TOPK)
    best_s, best_pos = lax.top_k(cand_s, PEER_TOPK)
    idx = jnp.take_along_axis(cand_i, best_pos, axis=-1).reshape(n, PEER_HEADS * PEER_TOPK)
    gates = jax.nn.softmax(best_s, axis=-1).reshape(n, PEER_HEADS * PEER_TOPK).astype(h.dtype)
    chunk = min(PEER_CHUNK, n)
    n_chunks = -(-n // chunk)
    pad = n_chunks * chunk - n
    xt_p = jnp.pad(xt, ((0, pad), (0, 0))).reshape(n_chunks, chunk, D)
    idx_p = jnp.pad(idx, ((0, pad), (0, 0))).reshape(n_chunks, chunk, PEER_HEADS * PEER_TOPK)
    g_p = jnp.pad(gates, ((0, pad), (0, 0))).reshape(n_chunks, chunk, PEER_HEADS * PEER_TOPK)

    def experts(args):
        xc, ic, gc = args
        act = jax.nn.gelu(jnp.einsum('cd,ced->ce', xc, u[ic]), approximate=False)
        return jnp.einsum('ce,ced->cd', gc * act, v[ic])

    out = lax.map(experts, (xt_p, idx_p, g_p)).reshape(n_chunks * chunk, D)[:n]
    return out.reshape(B, T, D)


def trunk_layer(x, c, pos, cache_k, cache_v, pool_hist, rel_table, ada_w, ada_b, norm1_g, w_in,
                q_norm_g, k_norm_g, sinks, pool_w, pool_scale, w_out, norm2_g,
                peer_w_query, peer_sub_keys, peer_u, peer_v):
    B, T = x.shape[:2]
    mod = (jax.nn.silu(c) @ ada_w + ada_b)[:, None, :]
    shift1, scale1, gate1, shift2, scale2, gate2 = jnp.split(mod, 6, axis=-1)
    h1 = rms_norm(x, norm1_g) * (1 + scale1) + shift1
    proj = h1 @ w_in
    cuts = [ATTN_WIDTH, ATTN_WIDTH + KV_WIDTH, ATTN_WIDTH + 2 * KV_WIDTH,
            ATTN_WIDTH + 2 * KV_WIDTH + POOL_WIDTH, ATTN_WIDTH + 2 * KV_WIDTH + POOL_WIDTH + D_MODEL]
    q, k, v, pin, gate_a, gate_b = jnp.split(proj, cuts, axis=-1)
    q = rms_norm(q.reshape(B, T, N_Q_HEADS, HEAD_DIM), q_norm_g)
    k = rms_norm(k.reshape(B, T, N_KV_HEADS, HEAD_DIM), k_norm_g)
    v = v.reshape(B, T, N_KV_HEADS, HEAD_DIM)
    if cache_k is None:
        attn = prompt_attention(q, k, v, rel_table, sinks)
        k_buf, v_buf = k[:, -WINDOW:], v[:, -WINDOW:]
        pool_hist = jnp.zeros((B, POOL_HIST, POOL_WIDTH), pin.dtype)
    else:
        attn, k_buf, v_buf = sample_attention(q, k, v, cache_k, cache_v, rel_table, sinks)
    p_ext = jnp.concatenate([pool_hist, pin], axis=1)
    pool = pool_mix(p_ext, pos, pool_w, pool_scale)
    mixed = jax.nn.sigmoid(gate_a) * attn + jax.nn.sigmoid(gate_b) * pool
    x = x + gate1 * (mixed @ w_out)
    h2 = rms_norm(x, norm2_g) * (1 + scale2) + shift2
    x = x + gate2 * peer_ffn(h2, peer_w_query, peer_sub_keys, peer_u, peer_v)
    return x, k_buf, v_buf, p_ext[:, -POOL_HIST:]


def setup_inputs(seed: int = 0) -> dict:
    key = jax.random.key(seed)
    ks = jax.random.split(key, 24)

    def nrm(k, shape, scale):
        return jax.random.normal(k, shape, jnp.float32) * scale

    D = D_MODEL
    return {
        "x_prompt": nrm(ks[0], (BATCH, SEQ, D), 1.0),
        "x_sample": nrm(ks[1], (DEC_BATCH, DEC_SEQ, D), 1.0),
        "cache_k_win": nrm(ks[2], (DEPTH, DEC_BATCH, WINDOW, N_KV_HEADS, HEAD_DIM), 1.0),
        "cache_v_win": nrm(ks[3], (DEPTH, DEC_BATCH, WINDOW, N_KV_HEADS, HEAD_DIM), 1.0),
        "state_pool": nrm(ks[4], (DEPTH, DEC_BATCH, POOL_HIST, POOL_WIDTH), 1.0),
        "c_prompt": nrm(ks[5], (BATCH, D), 1.0),
        "c_sample": nrm(ks[6], (DEC_BATCH, D), 1.0),
        "rel_bias_table": nrm(ks[7], (N_BUCKETS, N_Q_HEADS), 0.5),
        "ada_w": nrm(ks[8], (DEPTH, D, 6 * D), 0.3 * D ** -0.5),
        "ada_b": nrm(ks[9], (DEPTH, 6 * D), 0.02),
        "norm1_g": 1.0 + nrm(ks[10], (DEPTH, D), 0.1),
        "w_in": nrm(ks[11], (DEPTH, D, IN_WIDTH), D ** -0.5),
        "q_norm_g": 1.0 + nrm(ks[12], (DEPTH, HEAD_DIM), 0.1),
        "k_norm_g": 1.0 + nrm(ks[13], (DEPTH, HEAD_DIM), 0.1),
        "attn_sinks": nrm(ks[14], (DEPTH, N_Q_HEADS), 0.5),
        "pool_w": nrm(ks[15], (DEPTH, N_POOL_GROUPS, POOL_GROUP, POOL_OUT_GROUP), POOL_GROUP ** -0.5),
        "pool_scale": 1.0 + nrm(ks[16], (DEPTH, D), 0.1),
        "w_out": nrm(ks[17], (DEPTH, D, D), D ** -0.5),
        "norm2_g": 1.0 + nrm(ks[18], (DEPTH, D), 0.1),
        "peer_w_query": nrm(ks[19], (DEPTH, D, PEER_HEADS * PEER_KEY_DIM), D ** -0.5),
        "peer_sub_keys": nrm(ks[20], (DEPTH, PEER_HEADS, 2, PEER_N_KEYS, PEER_HALF), PEER_HALF ** -0.5),
        "peer_u": nrm(ks[21], (DEPTH, PEER_N_EXPERTS, D), D ** -0.5),
        "peer_v": nrm(ks[22], (DEPTH, PEER_N_EXPERTS, D), 0.5),
    }


def reference(x_prompt, x_sample, cache_k_win, cache_v_win, state_pool, c_prompt, c_sample,
              rel_bias_table, ada_w, ada_b, norm1_g, w_in, q_norm_g, k_norm_g, attn_sinks,
              pool_w, pool_scale, w_out, norm2_g, peer_w_query, peer_sub_keys, peer_u, peer_v):
    pos_p = jnp.arange(x_prompt.shape[1], dtype=jnp.int32)
    pos_s = PAST_LEN + jnp.arange(x_sample.shape[1], dtype=jnp.int32)
    xp, xs = x_prompt, x_sample
    kp, vp, pp, ks_, vs_, ps_ = [], [], [], [], [], []
    for l in range(DEPTH):
        xp, k_buf, v_buf, p_buf = trunk_layer(
            xp, c_prompt, pos_p, None, None, None, rel_bias_table, ada_w[l], ada_b[l], norm1_g[l],
            w_in[l], q_norm_g[l], k_norm_g[l], attn_sinks[l], pool_w[l], pool_scale[l], w_out[l],
            norm2_g[l], peer_w_query[l], peer_sub_keys[l], peer_u[l], peer_v[l])
        kp.append(k_buf); vp.append(v_buf); pp.append(p_buf)
        xs, k_buf, v_buf, p_buf = trunk_layer(
            xs, c_sample, pos_s, cache_k_win[l], cache_v_win[l], state_pool[l], rel_bias_table,
            ada_w[l], ada_b[l], norm1_g[l], w_in[l], q_norm_g[l], k_norm_g[l], attn_sinks[l],
            pool_w[l], pool_scale[l], w_out[l], norm2_g[l], peer_w_query[l], peer_sub_keys[l],
            peer_u[l], peer_v[l])
        ks_.append(k_buf); vs_.append(v_buf); ps_.append(p_buf)
    return (xp, xs, jnp.stack(kp), jnp.stack(vp), jnp.stack(pp), jnp.stack(ks_), jnp.stack(vs_), jnp.stack(ps_))
```

```python
import contextlib
import math
import numpy as np
import concourse.bass as bass
import concourse.mybir as mybir
from concourse.bass_utils import run_bass_kernel_spmd

F32 = mybir.dt.float32
BF16 = mybir.dt.bfloat16
I32 = mybir.dt.int32
U32 = mybir.dt.uint32
ALU = mybir.AluOpType
AF = mybir.ActivationFunctionType
AX = mybir.AxisListType

ENGS = ("pe", "dve", "act", "pool", "sp")
EPS = 1e-6
NT = 16
NS = 16
NSLOT = 13
POOL_W = (2, 4, 8, 16)
FILL_END = 124


class Op:
    __slots__ = ("eng", "fn", "deps", "is_dma", "inc_val", "dsem", "dval", "has_dep", "dprev", "line")

    def __init__(self, eng, fn, is_dma):
        self.eng = eng
        self.fn = fn
        self.is_dma = is_dma
        self.deps = []
        self.inc_val = None
        self.dsem = None
        self.dval = None
        self.dprev = None
        self.has_dep = False


class Prog:
    N_DMA_SEMS = 28
    N_HW_SEMS = 14
    PSUM_KEYS = frozenset(["q0", "q1", "o", "o0", "o1", "d", "t"])

    def __init__(self, nc):
        self.nc = nc
        self.ops = {e: [] for e in ENGS}
        self.last_w = {}
        self.readers = {}
        self.dma_i = 0
        self.dma_sw_i = 0
        self.dma_uses = [0] * self.N_DMA_SEMS
        self.last_dma_on_sem = [None] * self.N_DMA_SEMS

    def _add(self, eng, fn, reads, writes, is_dma, extra_deps=()):
        op = Op(eng, fn, is_dma)
        op.line = fn.__code__.co_firstlineno
        deps = set(extra_deps)
        xr = [k for k in reads if k in self.PSUM_KEYS]
        if xr:
            reads = [k for k in reads if k not in self.PSUM_KEYS]
            writes = list(writes) + xr
        for k in list(reads) + list(writes):
            w = self.last_w.get(k)
            if w is not None:
                deps.add(w)
        for k in writes:
            for r in self.readers.get(k, ()):
                deps.add(r)
        deps.discard(op)
        if eng == "pe" and not is_dma:
            deps = {d for d in deps if not (d.eng == "pe" and not d.is_dma)}
        op.deps = list(deps)
        for d in op.deps:
            d.has_dep = True
        for k in writes:
            self.last_w[k] = op
            self.readers[k] = []
        for k in reads:
            self.readers.setdefault(k, []).append(op)
        if is_dma:
            if eng == "pool":
                j = self.N_HW_SEMS + self.dma_sw_i % (self.N_DMA_SEMS - self.N_HW_SEMS)
                self.dma_sw_i += 1
            else:
                j = self.dma_i % self.N_HW_SEMS
                self.dma_i += 1
            op.dsem = j
            op.dprev = 16 * self.dma_uses[j]
            self.dma_uses[j] += 1
            op.dval = 16 * self.dma_uses[j]
            self.last_dma_on_sem[j] = op
        self.ops[eng].append(op)
        return op

    def op(self, eng, fn, reads=(), writes=()):
        return self._add(eng, fn, reads, writes, False)

    def dma(self, eng, fn, reads=(), writes=()):
        return self._add(eng, fn, reads, writes, True)

    def barrier(self):
        tails = []
        for e in ENGS:
            for o in reversed(self.ops[e]):
                if not o.is_dma:
                    tails.append(o)
                    break
        dmas = [o for o in self.last_dma_on_sem if o is not None]
        for e in ENGS:
            self._add(e, lambda eng: eng.nop(), (), (), False, extra_deps=tails + dmas)

    def emit(self):
        nc = self.nc
        for e in ENGS:
            c = 0
            for op in self.ops[e]:
                if not op.is_dma and op.has_dep:
                    c += 1
                    op.inc_val = c
        with contextlib.ExitStack() as st:
            esem = {e: st.enter_context(nc.semaphore("s_" + e)) for e in ENGS}
            dsems = [st.enter_context(nc.semaphore("d%d" % i)) for i in range(self.N_DMA_SEMS)]
            block = st.enter_context(nc.Block())
            prog = self

            def run(e, eng):
                waited = {}
                for op in prog.ops[e]:
                    need = {}
                    for d in op.deps:
                        if d.is_dma:
                            key = ("d", d.dsem)
                            v = d.dval
                        else:
                            key = ("e", d.eng)
                            v = d.inc_val
                        if need.get(key, 0) < v:
                            need[key] = v
                    if op.is_dma and op.dprev > 0:
                        key = ("d", op.dsem)
                        if need.get(key, 0) < op.dprev:
                            need[key] = op.dprev
                    for key, v in need.items():
                        if waited.get(key, 0) >= v:
                            continue
                        waited[key] = v
                        sem = dsems[key[1]] if key[0] == "d" else esem[key[1]]
                        eng.wait_ge(sem, v)
                    ins = op.fn(eng)
                    if op.is_dma:
                        ins.then_inc(dsems[op.dsem], 16)
                    elif op.inc_val is not None:
                        ins.then_inc(esem[e], 1)

            @block.tensor
            def _(eng):
                run("pe", eng)

            @block.vector
            def _(eng):
                run("dve", eng)

            @block.scalar
            def _(eng):
                run("act", eng)

            @block.gpsimd
            def _(eng):
                run("pool", eng)

            @block.sync
            def _(eng):
                run("sp", eng)


class Arena:
    def __init__(self, t, nwords):
        self.t = t
        self.n = nwords
        self.off = 0
        self.mark = 0

    def alloc(self, shape, dt=F32):
        n = 1
        for s in shape:
            n *= s
        words = n if dt in (F32, I32, U32) else (n + 1) // 2
        words = (words + 15) // 16 * 16
        assert self.off + words <= self.n, ("arena overflow", self.off, words, self.n)
        v = self.t[:, self.off:self.off + words]
        self.off += words
        if dt != F32:
            v = v.bitcast(dt)
        v = v[:, 0:n]
        if len(shape) == 1:
            return v
        names = " ".join("d%d" % i for i in range(len(shape)))
        kw = {"d%d" % i: shape[i] for i in range(len(shape))}
        return v.rearrange("p (%s) -> p %s" % (names, names), **kw)


def _consts():
    import jax
    import jax.numpy as jnp
    ident = np.eye(128, dtype=np.float32)
    with jax.default_device(jax.devices("cpu")[0]):
        d = jnp.arange(128)
        df = jnp.maximum(d, 1).astype(jnp.float32)
        large = 16 + (jnp.log(df / 16) / math.log(128 / 16) * 16).astype(jnp.int32)
        large = jnp.minimum(large, 31)
        bucket = np.asarray(jnp.where(d < 16, d, large))
    R = np.zeros((33, 384), np.float32)
    for j in range(384):
        m = 383 - j
        if 128 <= m < 256:
            R[bucket[m - 128], j] = 1.0
        else:
            R[32, j] = 1.0
    Mp = np.zeros((128, 3, 4, 128), np.float32)
    for g, w in enumerate(POOL_W):
        for t in range(128):
            for tp in range(max(0, t - w + 1), t + 1):
                Mp[tp, 0, g, t] += 1.0 / w
                Mp[tp, 1, g, t] += 1.0 / min(w, t + 1)
            Mp[t, 0, g, t] -= 1.0
            Mp[t, 1, g, t] -= 1.0
            for tp in range(128):
                if tp > 128 + t - w:
                    Mp[tp, 2, g, t] = 1.0 / w
    iota = np.tile(np.arange(16, dtype=np.float32)[None, :], (128, 1))
    p = np.arange(128)
    maskg = (p[:, None] // 16 == np.arange(8)[None, :]).astype(np.float32)
    esel = (p[:, None] % 16 == np.arange(16)[None, :]).astype(np.float32)
    return ident, R, Mp.reshape(128, 3 * 4 * 128), iota, maskg, esel


def build_nc(debug=False, stop=99, ntiles=NT, sub=99):
    nc = bass.Bass("TRN2", target_bir_lowering=False)

    def din(name, shape, dt=F32):
        return nc.dram_tensor(name, list(shape), dt, kind="ExternalInput").ap()

    def dout(name, shape, dt=F32):
        return nc.dram_tensor(name, list(shape), dt, kind="ExternalOutput").ap()

    xp = din("xp", [NT * 128, 1024]); xs = din("xs", [NS, 1024])
    ck = din("ck", [NS, 128, 128]); cv = din("cv", [NS, 128, 128]); spool = din("spool", [NS, 15, 512])
    c17 = din("c17", [17, 1024]); table = din("table", [32, 16])
    ada_w = din("ada_w", [1024, 6144]); ada_b = din("ada_b", [1, 6144]); g1 = din("g1", [1, 1024])
    w_in = din("w_in", [1024, 3840]); gq = din("gq", [1, 64]); gk = din("gk", [1, 64]); sinks = din("sinks", [1, 16])
    pool_w = din("pool_w", [4, 128, 256]); pool_scale = din("pool_scale", [1, 1024]); w_out = din("w_out", [1024, 1024])
    g2 = din("g2", [1, 1024]); w_query = din("w_query", [1024, 2048]); sub_keys = din("sub_keys", [16, 128, 128])
    peer_u = din("peer_u", [16384, 1024]); peer_v = din("peer_v", [16384, 1024])
    c_ident = din("c_ident", [128, 128]); c_R = din("c_R", [33, 384]); c_Mp = din("c_Mp", [128, 1536]); c_iota = din("c_iota", [128, 16])
    c_maskg = din("c_maskg", [128, 8]); c_esel = din("c_esel", [128, 16])

    yp = dout("yp", [NT * 128, 1024]); ys = dout("ys", [NS, 1024])
    kwp = dout("kwp", [128, 128]); vwp = dout("vwp", [128, 128]); poolp = dout("poolp", [15, 512])
    kws = dout("kws", [NS, 128, 128]); vws = dout("vws", [NS, 128, 128]); pools = dout("pools", [NS, 15, 512])
    out_keys = []

    modS = nc.dram_tensor("modS", [17, 6, 1024], F32, kind="Internal").ap()
    x1s = nc.dram_tensor("x1s", [(NT + 1) * 128, 1024], F32, kind="Internal").ap()
    krow = nc.dram_tensor("krow", [2, NS, 128], F32, kind="Internal").ap()
    uvb = nc.dram_tensor("uvb", [16384, 2048], BF16, kind="Internal").ap()

    P = Prog(nc)
    global _LASTP
    _LASTP = P
    dbg = {}

    rec = [None]

    def _emit(eng, fn, r, w, dma=False):
        r, w = list(r), list(w)
        if rec[0] is not None:
            rec[0].append((eng if not dma else "dma", lambda: (P.dma if dma else P.op)(eng, fn, r, w)))
            return None
        return (P.dma if dma else P.op)(eng, fn, r, w)

    def V(fn, r=(), w=()):
        return _emit("dve", fn, r, w)

    def A(fn, r=(), w=()):
        return _emit("act", fn, r, w)

    def T(fn, r=(), w=()):
        return _emit("pe", fn, r, w)

    def G(fn, r=(), w=()):
        return _emit("pool", fn, r, w)

    def D(out, in_, r=(), w=(), q="sp"):
        return _emit(q, lambda e: e.dma_start(out=out, in_=in_), r, w, dma=True)

    def bcast_rows(ap2d, n):
        return bass.AP(tensor=ap2d.tensor, offset=ap2d.offset, ap=[[0, n]] + [list(x) for x in ap2d.ap[1:]])

    with contextlib.ExitStack() as st:
        A0_WORDS = 5888
        A1_WORDS = 47300
        a0t = st.enter_context(nc.sbuf_tensor("arena0", [128, A0_WORDS], F32))
        a1t = st.enter_context(nc.sbuf_tensor("arena1", [128, A1_WORDS], F32))
        psQ = st.enter_context(nc.psum_tensor("psQ", [128, 4, 512], F32))
        psO = st.enter_context(nc.psum_tensor("psO", [128, 2, 512], F32))
        psD = st.enter_context(nc.psum_tensor("psD", [128, 512], F32))
        psT = st.enter_context(nc.psum_tensor("psT", [128, 8, 128], BF16))
        A0 = Arena(a0t, A0_WORDS)
        A1 = Arena(a1t, A1_WORDS)

        def dump(name, ap, shape, reads, dt=F32):
            if not debug:
                return
            o = dout("dbg_" + name, shape, dt)
            D(o, ap, reads, ["dbg_" + name])
            out_keys.append("dbg_" + name)

        ident_f = A0.alloc([128], F32)
        ident_b = A0.alloc([128], BF16)
        ones_b = A0.alloc([8], BF16)
        ones_f = A0.alloc([32], F32)
        gqs = A0.alloc([64]); gkb = A0.alloc([64]); esink = A0.alloc([16]); iota16 = A0.alloc([16])
        xt = A0.alloc([1024]); hf = A0.alloc([1024]); hb = A0.alloc([1024], BF16); hT = A0.alloc([8, 128], BF16)
        F1 = A0.alloc([1024]); F2 = A0.alloc([1024])
        st_small = A0.alloc([64])
        vf = A0.alloc([128]); knf = A0.alloc([128])
        stage = [A1.alloc([2048]) for _ in range(2)]
        A1.mark = A1.off

        D(ident_f, c_ident, [], ["ident_f"])
        A(lambda e: e.copy(out=ident_b, in_=ident_f), ["ident_f"], ["ident_b"])
        G(lambda e: e.memset(ones_b, 1.0), [], ["ones_b"])
        G(lambda e: e.memset(ones_f, 1.0), [], ["ones_f"])
        D(gqs, bcast_rows(gq, 128), [], ["gqs"])
        V(lambda e: e.tensor_scalar(out=gqs, in0=gqs, scalar1=0.125, scalar2=0.0, op0=ALU.mult, op1=ALU.add), ["gqs"], ["gqs"])
        D(gkb, bcast_rows(gk, 128), [], ["gkb"])
        D(esink, bcast_rows(sinks, 128), [], ["esink"])
        A(lambda e: e.activation(out=esink, in_=esink, func=AF.Exp), ["esink"], ["esink"])
        D(iota16, c_iota, [], ["iota16"])

        M1 = A1.alloc([3, 1024])
        w_in_b = A1.alloc([8, 3840], BF16)
        w_out_b = A1.alloc([8, 1024], BF16)
        pw_b = A1.alloc([4, 256], BF16)
        bias = [A1.alloc([16, 128]) for _ in range(2)]
        Mp = A1.alloc([3, 4, 128])
        R_sb = A1.alloc([384])
        tab_sb = A1.alloc([16])
        qf = A1.alloc([1024]); qn = A1.alloc([8, 2, 64], BF16); qTP = [A1.alloc([8, 128], BF16) for _ in range(2)]; qT = qTP[0]
        pin = [A1.alloc([512]) for _ in range(3)]
        knb = A1.alloc([128], BF16)
        kT = [A1.alloc([128], BF16) for _ in range(3)]
        Vb = [A1.alloc([128], BF16) for _ in range(3)]
        PT = A1.alloc([2, 16, 128], BF16)
        xtB = A1.alloc([1024]); F3 = A1.alloc([1024])
        rTb = A1.alloc([4, 128], BF16)
        qTs = A1.alloc([16, 8], BF16)
        off_u = A1.off
        c17t = A1.alloc([1024]); cT = A1.alloc([8, 17], BF16); gbc = [A1.alloc([1024]) for _ in range(2)]
        mch = [A1.alloc([512]) for _ in range(2)]; abt = [A1.alloc([512], BF16) for _ in range(2)]; ones_r = A1.alloc([32], BF16)
        A1.off = off_u
        sgaP = [A1.alloc([1024], BF16) for _ in range(2)]; sgbP = [A1.alloc([1024], BF16) for _ in range(2)]
        sbuf_s = A1.alloc([1024])
        att = A1.alloc([1024]); mixb = A1.alloc([1024], BF16); mixT = A1.alloc([8, 128], BF16)
        b0f = bias[0].rearrange("p h q -> p (h q)")
        kall_b = b0f[:, 0:1024].bitcast(BF16).rearrange("p (b c) -> p b c", c=128)
        vall_b = b0f[:, 1024:2048].bitcast(BF16).rearrange("p (b c) -> p b c", c=128)
        kTs = Mp.rearrange("p a g t -> p (a g t)")[:, 0:1024].bitcast(BF16).rearrange("p (b c) -> p b c", c=128)

        D(R_sb[0:33, :], c_R, [], ["R_sb"])
        D(tab_sb[0:32, :], table, [], ["tab_sb"])
        G(lambda e: e.memset(tab_sb[32:33, :], -30000.0), [], ["tab32"])
        def build_bias():
            bi = 0
            for blk in range(2):
                for qc in range(4):
                    bank = bi % 4
                    pk = "q%d" % (bank // 2)
                    bi += 1
                    for ql in range(32):
                        q = qc * 32 + ql
                        off = (255 - q) if blk == 1 else (127 - q)
                        T(lambda e, bank=bank, ql=ql, off=off: e.matmul(psQ[:, bank, ql * 16:(ql + 1) * 16], lhsT=R_sb[0:33, off:off + 128],
                                                                       rhs=tab_sb[0:33, :], start=True, stop=True),
                          ["R_sb", "tab_sb", "tab32"], [pk])
                    dst = bias[blk][:, :, qc * 32:(qc + 1) * 32]
                    src = psQ[:, bank, :].rearrange("p (q h) -> p h q", h=16)
                    if bi % 2 == 0:
                        A(lambda e, dst=dst, src=src: e.copy(out=dst, in_=src), [pk], ["bias"])
                    else:
                        V(lambda e, dst=dst, src=src: e.tensor_copy(out=dst, in_=src), [pk], ["bias"])


        build_bias()

        for kc in range(8):
            P.dma("pool", lambda e, kc=kc: e.dma_start(out=w_in_b[:, kc, :], in_=w_in[kc * 128:(kc + 1) * 128, :]), [], ["wts"])
        for kc in range(8):
            P.dma("pool", lambda e, kc=kc: e.dma_start(out=w_out_b[:, kc, :], in_=w_out[kc * 128:(kc + 1) * 128, :]), [], ["wts"])
        D(c17t[0:17, :], c17, [], ["c17t"])
        A(lambda e: e.activation(out=c17t[0:17, :], in_=c17t[0:17, :], func=AF.Silu), ["c17t"], ["c17t"])
        G(lambda e: e.memset(ones_r, 1.0), [], ["ones_r"])
        for kc in range(8):
            T(lambda e, kc=kc: e.transpose(out=psO[:, 0, kc * 32:kc * 32 + 17], in_=c17t[0:17, kc * 128:(kc + 1) * 128],
                                           identity=ident_f[0:17, 0:17]), ["c17t", "ident_f"], ["o"])
        A(lambda e: e.copy(out=cT, in_=psO[:, 0, 0:256].rearrange("p (k c) -> p k c", c=32)[:, :, 0:17]), ["o"], ["cT"])
        D(gbc[0][0:17, :], bcast_rows(g1, 17), [], ["gbc0"])
        D(gbc[1][0:17, :], bcast_rows(g2, 17), [], ["gbc1"])
        for cc in range(12):
            sgb_ = stage[cc % 2].bitcast(BF16)
            sk = "stage%d" % (cc % 2)
            seg, co = cc // 2, (cc % 2) * 512
            P.dma("pool", lambda e, sgb_=sgb_, cc=cc: e.dma_start(
                out=sgb_.rearrange("p (k c) -> p k c", c=512),
                in_=ada_w[:, cc * 512:(cc + 1) * 512].rearrange("(k p) c -> p k c", p=128)), [], [sk])
            P.dma("pool", lambda e, cc=cc: e.dma_start(out=abt[cc % 2][0:1, :], in_=ada_b[:, cc * 512:(cc + 1) * 512]), [], ["abt%d" % (cc % 2)])
            pk = "q%d" % (cc % 2)
            pso = psQ[0:17, (cc % 2) * 2, 0:512]
            for kc in range(8):
                T(lambda e, kc=kc, sgb_=sgb_, pso=pso: e.matmul(pso, lhsT=cT[:, kc, :], rhs=sgb_[:, kc * 512:(kc + 1) * 512],
                                                                start=(kc == 0), stop=False), ["cT", sk], [pk])
            T(lambda e, pso=pso, cc=cc: e.matmul(pso, lhsT=ones_r[0:1, 0:17], rhs=abt[cc % 2][0:1, :], start=False, stop=True),
              ["ones_r", "abt%d" % (cc % 2)], [pk])
            mc = mch[cc % 2]
            mk = "mch%d" % (cc % 2)
            if seg in (1, 4):
                gb_ = gbc[0] if seg == 1 else gbc[1]
                V(lambda e, mc=mc, pso=pso, gb_=gb_, co=co: e.scalar_tensor_tensor(
                    out=mc[0:17, :], in0=pso, scalar=1.0, in1=gb_[0:17, co:co + 512], op0=ALU.add, op1=ALU.mult),
                  [pk, "gbc0", "gbc1"], [mk])
            else:
                A(lambda e, mc=mc, pso=pso: e.copy(out=mc[0:17, :], in_=pso), [pk], [mk])
            D(modS[:, seg, co:co + 512], mc[0:17, :], [mk], ["modS"])

        cnt = [0]

        def cvt(dst, src_dram, ncols):
            i = cnt[0]
            cnt[0] += 1
            sg = stage[i % 2]
            sk = "stage%d" % (i % 2)
            D(sg[:, 0:ncols], src_dram, [], [sk])
            if i % 2 == 0:
                A(lambda e: e.copy(out=dst, in_=sg[:, 0:ncols]), [sk], ["wts"])
            else:
                V(lambda e: e.tensor_copy(out=dst, in_=sg[:, 0:ncols]), [sk], ["wts"])

        D(stage[0][:, 0:1024].rearrange("p (g o) -> p g o", g=4), pool_w.rearrange("g c o -> c g o"), [], ["stage0"])
        D(stage[1][:, 0:1024], bcast_rows(pool_scale, 128), [], ["stage1"])
        V(lambda e: e.tensor_tensor(out=pw_b.rearrange("p g o -> p (g o)"), in0=stage[0][:, 0:1024], in1=stage[1][:, 0:1024], op=ALU.mult),
          ["stage0", "stage1"], ["wts"])
        D(Mp.rearrange("p a g t -> p (a g t)"), c_Mp, [], ["Mp"])

        D(M1, bass.AP(tensor=modS.tensor, offset=modS[0:1, 0:3, :].offset, ap=[[0, 128], [1024, 3], [1, 1024]]), ["modS"], ["M1"])
        P.barrier()

        for hi_, src_t in enumerate((peer_u, peer_v)):
            for k in range(8):
                rows = slice(k * 2048, (k + 1) * 2048)
                P.dma("pool", lambda e, src_t=src_t, rows=rows, hi_=hi_: e.dma_start(
                    out=uvb[rows, hi_ * 1024:(hi_ + 1) * 1024], in_=src_t[rows, :]), [], [("tabs", hi_, k)])
        tabs_keys = [("tabs", hi_, k) for hi_ in range(2) for k in range(8)]

        def rms_mod(n, M, Mk, want_f32, xt=xt, xk="xt"):
            r = slice(0, n)
            ss, ms, rs = st_small[r, 0:1], st_small[r, 1:2], st_small[r, 2:3]
            A(lambda e: e.activation(out=F1[r, :], in_=xt[r, :], func=AF.Square, accum_out=ss), [xk], ["F1", "ss"])
            V(lambda e: e.tensor_scalar(out=ms, in0=ss, scalar1=1.0 / 1024, scalar2=EPS, op0=ALU.mult, op1=ALU.add), ["ss"], ["ms"])
            A(lambda e: e.activation(out=ms, in_=ms, func=AF.Sqrt), ["ms"], ["ms"])
            V(lambda e: e.reciprocal(out=rs, in_=ms), ["ms"], ["rs"])
            V(lambda e: e.scalar_tensor_tensor(out=F2[r, :], in0=xt[r, :], scalar=rs, in1=M[r, 1, :], op0=ALU.mult, op1=ALU.mult),
              [xk, "rs", Mk], ["F2"])
            if want_f32:
                V(lambda e: e.tensor_tensor(out=hf[r, :], in0=F2[r, :], in1=M[r, 0, :], op=ALU.add), ["F2", Mk], ["hf"])
                A(lambda e: e.copy(out=hb[r, :], in_=hf[r, :]), ["hf"], ["hb"])
            else:
                V(lambda e: e.tensor_tensor(out=hb[r, :], in0=F2[r, :], in1=M[r, 0, :], op=ALU.add), ["F2", Mk], ["hb"])

        psDb = psD[:, :].bitcast(BF16).rearrange("p (k c) -> p k c", c=128)

        def transpose8(n, src, srck, dst, dstk, ps=psT, psk="t"):
            for kc in range(8):
                T(lambda e, kc=kc: e.transpose(out=ps[:, kc, 0:n], in_=src[0:n, kc * 128:(kc + 1) * 128], identity=ident_b[0:n, 0:n]),
                  [srck, "ident_b"], [psk])
            A(lambda e: e.copy(out=dst[:, :, 0:n], in_=ps[:, :, 0:n]), [psk], [dstk])

        def proj(n, c0, c1, outs, pk):
            c = c0
            for (pso, nc_) in outs:
                for kc in range(8):
                    T(lambda e, kc=kc, pso=pso, c=c, nc_=nc_: e.matmul(pso, lhsT=hT[:, kc, 0:n], rhs=w_in_b[:, kc, c:c + nc_],
                                                                       start=(kc == 0), stop=(kc == 7)), ["hT", "wts"], [pk])
                c += nc_
            assert c == c1

        def head_rstd(n, src_ps, pk, nh, col0):
            r = slice(0, n)
            sq = F1[r, 0:nh * 64]
            ssq = st_small[r, col0:col0 + nh]
            A(lambda e: e.activation(out=sq, in_=src_ps, func=AF.Square), [pk], ["F1"])
            V(lambda e: e.tensor_reduce(out=ssq, in_=sq.rearrange("p (h d) -> p h d", d=64), axis=AX.X, op=ALU.add), ["F1"], ["hr"])
            V(lambda e: e.tensor_scalar(out=ssq, in0=ssq, scalar1=1.0 / 64, scalar2=EPS, op0=ALU.mult, op1=ALU.add), ["hr"], ["hr"])
            A(lambda e: e.activation(out=ssq, in_=ssq, func=AF.Sqrt), ["hr"], ["hr"])
            V(lambda e: e.reciprocal(out=ssq, in_=ssq), ["hr"], ["hr"])
            return ssq

        def front(n, par, gi=0, X=xt, XK="xt"):
            qT, sga, sgb = qTP[gi], sgaP[gi], sgbP[gi]
            r = slice(0, n)
            rms_mod(n, M1, "M1", False, X, XK)
            transpose8(n, hb, "hb", hT, "hT")
            proj(n, 0, 1024, [(psQ[r, 0, :], 512), (psQ[r, 1, :], 512)], "q0")
            qps = psQ[r, 0:2, :].rearrange("p a (h d) -> p (a h) d", d=64)
            rq = head_rstd(n, psQ[r, 0:2, :].rearrange("p a c -> p (a c)"), "q0", 16, 4)
            V(lambda e: e.tensor_tensor(out=qf[r, :].rearrange("p (h d) -> p h d", d=64), in0=qps,
                                        in1=rq.unsqueeze(2).to_broadcast([n, 16, 64]), op=ALU.mult), ["q0", "hr"], ["qf"])
            V(lambda e: e.tensor_tensor(out=qn[r].rearrange("p hh g d -> p g hh d"),
                                        in0=qf[r, :].rearrange("p (g hh d) -> p g hh d", g=2, hh=8),
                                        in1=gqs[r, :].unsqueeze(1).unsqueeze(1).to_broadcast([n, 2, 8, 64]), op=ALU.mult),
              ["qf", "gqs"], ["qn"])
            for hh in range(8):
                T(lambda e, hh=hh: e.transpose(out=psT[:, hh, 0:n], in_=qn[r, hh, :, :].rearrange("p g d -> p (g d)"),
                                               identity=ident_b[0:n, 0:n]), ["qn", "ident_b"], ["t"])
            A(lambda e: e.copy(out=qT[:, :, 0:n], in_=psT[:, :, 0:n]), ["t"], ["qT%d" % gi])
            proj(n, 1024, 1792, [(psQ[r, 0, :], 512), (psQ[r, 1, 0:256], 256)], "q0")
            A(lambda e: e.copy(out=pin[par][r, 0:256], in_=psQ[r, 0, 256:512]), ["q0"], ["pin%d" % par])
            A(lambda e: e.copy(out=pin[par][r, 256:512], in_=psQ[r, 1, 0:256]), ["q0"], ["pin%d" % par])
            A(lambda e: e.copy(out=vf[r, :], in_=psQ[r, 0, 128:256]), ["q0"], ["vf"])
            V(lambda e: e.tensor_copy(out=Vb[par][r, :], in_=psQ[r, 0, 128:256]), ["q0"], ["Vb%d" % par])
            rk = head_rstd(n, psQ[r, 0, 0:128], "q0", 2, 24)
            V(lambda e: e.tensor_tensor(out=knf[r, :].rearrange("p (h d) -> p h d", d=64),
                                        in0=psQ[r, 0, 0:128].rearrange("p (h d) -> p h d", d=64),
                                        in1=rk.unsqueeze(2).to_broadcast([n, 2, 64]), op=ALU.mult), ["q0", "hr"], ["knf"])
            V(lambda e: e.tensor_tensor(out=knf[r, :].rearrange("p (h d) -> p h d", d=64),
                                        in0=knf[r, :].rearrange("p (h d) -> p h d", d=64),
                                        in1=gkb[r, :].unsqueeze(1).to_broadcast([n, 2, 64]), op=ALU.mult), ["knf", "gkb"], ["knf"])
            proj(n, 1792, 2816, [(psQ[r, 0, :], 512), (psQ[r, 1, :], 512)], "q0")
            A(lambda e: e.activation(out=sga[r, :], in_=psQ[r, 0:2, :].rearrange("p a c -> p (a c)"), func=AF.Sigmoid), ["q0"], ["sga%d" % gi])
            proj(n, 2816, 3840, [(psQ[r, 0, :], 512), (psQ[r, 1, :], 512)], "q0")
            A(lambda e: e.activation(out=sgb[r, :], in_=psQ[r, 0:2, :].rearrange("p a c -> p (a c)"), func=AF.Sigmoid), ["q0"], ["sgb%d" % gi])

        def attn_finalize(n, gi=0):
            r = slice(0, n)
            sga = sgaP[gi]
            den = st_small[r, 32:48]
            V(lambda e: e.tensor_tensor(out=den, in0=psD[r, 256:272], in1=esink[r, :], op=ALU.add), ["d", "esink"], ["den"])
            V(lambda e: e.reciprocal(out=den, in_=den), ["den"], ["den"])
            V(lambda e: e.tensor_tensor(out=att[r, :].rearrange("p (h d) -> p h d", d=64),
                                        in0=psO[r, :, :].rearrange("p a (h d) -> p (a h) d", d=64),
                                        in1=den.unsqueeze(2).to_broadcast([n, 16, 64]), op=ALU.mult), ["o", "den"], ["att"])
            V(lambda e: e.tensor_tensor(out=att[r, :], in0=att[r, :], in1=sga[r, :], op=ALU.mult), ["att", "sga%d" % gi], ["att"])

        def back(n, x1_dst, X=xt, XK="xt", gi=0):
            r = slice(0, n)
            F2 = F3
            sgb = sgbP[gi]
            for g in range(4):
                T(lambda e, g=g: e.matmul(psQ[r, 2 + g // 2, (g % 2) * 256:(g % 2) * 256 + 256], lhsT=rTb[:, g, 0:n], rhs=pw_b[:, g, :],
                                          start=True, stop=True), ["rTb", "wts"], ["q1"])
            V(lambda e: e.tensor_tensor(out=F2[r, :], in0=psQ[r, 2:4, :].rearrange("p a c -> p (a c)"), in1=sgb[r, :], op=ALU.mult),
              ["q1", "sgb%d" % gi], ["F3"])
            V(lambda e: e.tensor_tensor(out=mixb[r, :], in0=F2[r, :], in1=att[r, :], op=ALU.add), ["F3", "att"], ["mixb"])
            transpose8(n, mixb, "mixb", mixT, "mixT", psDb, "d")
            for half in range(2):
                for kc in range(8):
                    T(lambda e, half=half, kc=kc: e.matmul(psQ[r, 2 + half, :], lhsT=mixT[:, kc, 0:n], rhs=w_out_b[:, kc, half * 512:(half + 1) * 512],
                                                           start=(kc == 0), stop=(kc == 7)), ["mixT", "wts"], ["q1"])
            V(lambda e: e.tensor_tensor(out=F2[r, :], in0=psQ[r, 2:4, :].rearrange("p a c -> p (a c)"), in1=M1[r, 2, :], op=ALU.mult),
              ["q1", "M1"], ["F3"])
            V(lambda e: e.tensor_tensor(out=F2[r, :], in0=F2[r, :], in1=X[r, :], op=ALU.add), ["F3", XK], ["F3"])
            D(x1_dst, F2[r, :], ["F3"], ["x1s"])

        def prompt_front(t):
            par = t % 3
            gi = t % 2
            X, XK = (xt, "xt") if gi == 0 else (xtB, "xtB")
            D(X, xp[t * 128:(t + 1) * 128, :], [], [XK])
            front(128, par, gi, X, XK)
            A(lambda e: e.copy(out=knb, in_=knf), ["knf"], ["knb"])
            T(lambda e: e.transpose(out=psT[:, 0, :], in_=knb, identity=ident_b), ["knb", "ident_b"], ["t"])
            A(lambda e: e.copy(out=kT[par], in_=psT[:, 0, :]), ["t"], ["kT%d" % par])
            if t == NT - 1:
                D(kwp, knf, ["knf"], ["o_kwp"]); D(vwp, vf, ["vf"], ["o_vwp"])
                D(poolp, pin[par][113:128, :], ["pin%d" % par], ["o_poolp"])

        def prompt_back(t):
            par, prv = t % 3, (t - 1) % 3
            gi = t % 2
            qT = qTP[gi]
            X, XK = (xt, "xt") if gi == 0 else (xtB, "xtB")
            blks = ([0] if t > 0 else []) + [1]
            i = 0
            for blk in blks:
                ksrc, kk = (kT[par], "kT%d" % par) if blk == 1 else (kT[prv], "kT%d" % prv)
                for g in range(2):
                    bank = 2
                    pk = "q1"
                    i += 1
                    for half in range(2):
                        T(lambda e, g=g, half=half, bank=bank, ksrc=ksrc: e.matmul(
                            psQ[:, bank + half, :], lhsT=ksrc[g * 64:(g + 1) * 64, :],
                            rhs=qT[g * 64:(g + 1) * 64, half * 4:(half + 1) * 4, :].rearrange("p a t -> p (a t)"), start=True, stop=True), [kk, "qT%d" % gi], [pk])
                    V(lambda e, g=g, bank=bank, blk=blk: e.tensor_tensor(
                        out=sbuf_s.rearrange("p (h q) -> p h q", q=128),
                        in0=psQ[:, bank:bank + 2, :].rearrange("p a (h q) -> p (a h) q", q=128),
                        in1=bias[blk][:, g * 8:(g + 1) * 8, :], op=ALU.add), [pk, "bias"], ["sbuf_s"])
                    A(lambda e, g=g, blk=blk: e.activation(out=PT[:, blk, g * 8:(g + 1) * 8, :], in_=sbuf_s.rearrange("p (h q) -> p h q", q=128),
                                                          func=AF.Exp), ["sbuf_s"], ["PT"])
            for h in range(16):
                g = h // 8
                oreg = psO[:, h // 8, (h % 8) * 64:(h % 8) * 64 + 64]
                for bi_, blk in enumerate(blks):
                    vsrc, vk = (Vb[par], "Vb%d" % par) if blk == 1 else (Vb[prv], "Vb%d" % prv)
                    T(lambda e, h=h, g=g, blk=blk, bi_=bi_, vsrc=vsrc, oreg=oreg: e.matmul(
                        oreg, lhsT=PT[:, blk, h, :], rhs=vsrc[:, g * 64:(g + 1) * 64], start=(bi_ == 0), stop=(bi_ == len(blks) - 1)),
                      ["PT", vk], ["o"])
                for bi_, blk in enumerate(blks):
                    T(lambda e, h=h, blk=blk, bi_=bi_: e.matmul(psD[:, 256 + h:257 + h], lhsT=PT[:, blk, h, :], rhs=ones_b[:, 0:1],
                                                               start=(bi_ == 0), stop=(bi_ == len(blks) - 1)), ["PT", "ones_b"], ["d"])
            attn_finalize(128, gi)
            kind = 1 if t == 0 else 0
            for g in range(4):
                T(lambda e, g=g: e.matmul(psQ[:, 2, g * 128:(g + 1) * 128], lhsT=pin[par][:, g * 128:(g + 1) * 128], rhs=Mp[:, kind, g, :],
                                          start=True, stop=(t == 0)), ["pin%d" % par, "Mp"], ["q1"])
                if t > 0:
                    T(lambda e, g=g: e.matmul(psQ[:, 2, g * 128:(g + 1) * 128], lhsT=pin[prv][:, g * 128:(g + 1) * 128], rhs=Mp[:, 2, g, :],
                                              start=False, stop=True), ["pin%d" % prv, "Mp"], ["q1"])
            A(lambda e: e.copy(out=rTb.rearrange("p g t -> p (g t)"), in_=psQ[:, 2, :]), ["q1"], ["rTb"])
            back(128, x1s[t * 128:(t + 1) * 128, :], X, XK, gi)

        def record(fn, *a):
            rec[0] = []
            fn(*a)
            raw, rec[0] = rec[0], None
            out, run = [], []

            def flush():
                if run:
                    grp = list(run)
                    out.append(lambda: [g() for g in grp])
                    run.clear()
            for eng, th in raw:
                if eng == "pe":
                    run.append(th)
                else:
                    flush()
                    out.append(th)
            flush()
            return out

        np1 = ntiles if stop >= 2 else 0
        if np1:
            for f_ in record(prompt_front, 0):
                f_()
        for t in range(np1):
            bk = record(prompt_back, t)
            fr = record(prompt_front, t + 1) if t + 1 < np1 else []
            i = j = 0
            while i < len(bk) or j < len(fr):
                if j < len(fr) and (i >= len(bk) or j * len(bk) <= i * len(fr)):
                    fr[j]()
                    j += 1
                else:
                    bk[i]()
                    i += 1

        def sample_tile():
            n = NS
            r = slice(0, n)
            D(M1[r], modS[1:17, 0:3, :], ["modS"], ["M1"])
            D(xt[r, :], xs, [], ["xt"])
            front(n, 0, 0)
            if sub < 2:
                return
            kall_f = stage[0].rearrange("p (b c) -> p b c", c=128)
            vall_f = stage[1].rearrange("p (b c) -> p b c", c=128)
            import os as _os
            KD = _os.environ.get("KDIS", "")
            if "a" not in KD: D(kall_f[0:112, :, :], ck[:, 1:113, :].rearrange("b k c -> k b c"), [], ["stage0"])
            if "b" not in KD: D(kall_f[112:127, :, :], ck[:, 113:128, :].rearrange("b k c -> k b c"), [], ["stage0c"])
            if "c" not in KD:
                A(lambda e: e.copy(out=F2[r, 0:128], in_=knf[r, :]), ["knf"], ["F2"])
                D(krow[1], F2[r, 0:128], ["F2"], ["krow0"])
                if "C" not in KD: D(kall_f[127:128, :, :], krow[1], ["krow0", "stage0"], ["stage0b"])
            if "d" not in KD: D(vall_f[0:112, :, :], cv[:, 1:113, :].rearrange("b k c -> k b c"), [], ["stage1"])
            if "e" not in KD: D(vall_f[112:127, :, :], cv[:, 113:128, :].rearrange("b k c -> k b c"), [], ["stage1c"])
            if "f" not in KD:
                D(krow[0], vf[r, :], ["vf"] + (["knf"] if "X" in KD else []), ["krow1"])
                if "F" not in KD: D(vall_f[127:128, :, :], krow[0], ["krow1", "stage1"], ["stage1b"])
            if "g" not in KD: D(kws.rearrange("b k c -> k b c"), kall_f, ["stage0", "stage0b", "stage0c"], ["o_kws"])
            if "h" not in KD: D(vws.rearrange("b k c -> k b c"), vall_f, ["stage1", "stage1b", "stage1c"], ["o_vws"])
            out_keys.extend(["o_kws", "o_vws"])
            if sub < 3:
                return
            A(lambda e: e.copy(out=kall_b, in_=kall_f), ["stage0", "stage0b", "stage0c"], ["kall_b", "bias"])
            V(lambda e: e.tensor_copy(out=vall_b, in_=vall_f), ["stage1", "stage1b", "stage1c"], ["vall_b", "bias"])
            KSS = _os.environ.get("KSS", "z")
            if KSS == "a":
                return
            for rd in range(2):
                for bb in range(8):
                    b = rd * 8 + bb
                    T(lambda e, b=b, bb=bb: e.transpose(out=psT[:, bb, :], in_=kall_b[:, b, :], identity=ident_b), ["kall_b", "ident_b"], ["t"])
                A(lambda e, rd=rd: e.copy(out=kTs[:, rd * 8:(rd + 1) * 8, :], in_=psT[:, :, :]), ["t"], ["kTs", "Mp"])
            if KSS == "b":
                return
            A(lambda e: e.copy(out=qTs, in_=qT[:, :, 0:n].rearrange("p hh b -> p b hh")), ["qT0"], ["qTs"])
            for b in range(n):
                for g in range(2):
                    T(lambda e, b=b, g=g: e.matmul(psD[:, b * 16 + g * 8:b * 16 + g * 8 + 8], lhsT=kTs[g * 64:(g + 1) * 64, b, :],
                                                   rhs=qTs[g * 64:(g + 1) * 64, b, :], start=True, stop=True), ["kTs", "qTs"], ["d"])
            if KSS == "c":
                return
            sbs = sbuf_s[:, 0:256].rearrange("p (b h) -> p b h", h=16)
            V(lambda e: e.tensor_tensor(out=sbs, in0=psD[:, 0:256].rearrange("p (b h) -> p b h", h=16),
                                        in1=bias[1][:, :, 127].unsqueeze(1).to_broadcast([128, 16, 16]), op=ALU.add), ["d", "bias"], ["sbuf_s"])
            if sub < 4:
                return
            PTf = PT.rearrange("p a h q -> p (a h q)")
            G(lambda e: e.memset(PTf, 0.0), [], ["PT"])
            diag = bass.AP(tensor=PTf.tensor, offset=PTf.offset, ap=[list(PTf.ap[0]), [257, 16], [16, 16]])
            A(lambda e: e.activation(out=diag, in_=sbs, func=AF.Exp), ["sbuf_s", "PT"], ["PT"])
            if sub < 5:
                return
            PTp = PTf.rearrange("p (b h c) -> p b h c", h=16, c=16)
            for h in range(16):
                g = h // 8
                oreg = psO[r, h // 8, (h % 8) * 64:(h % 8) * 64 + 64]
                for b in range(n):
                    T(lambda e, h=h, g=g, b=b, oreg=oreg: e.matmul(oreg, lhsT=PTp[:, b, h, :], rhs=vall_b[:, b, g * 64:(g + 1) * 64],
                                                                   start=(b == 0), stop=(b == n - 1)), ["PT", "vall_b"], ["o"])
                for b in range(n):
                    T(lambda e, h=h, b=b: e.matmul(psD[r, 256 + h:257 + h], lhsT=PTp[:, b, h, :], rhs=ones_b[:, 0:1],
                                                   start=(b == 0), stop=(b == n - 1)), ["PT", "ones_b"], ["d"])
            attn_finalize(n, 0)
            if sub < 6:
                return
            pin_s = pin[0]
            tot = F1
            hoff = 0
            stage_all = a1t[:, 0:4096]
            stk = ["stage0", "stage0b", "stage0c", "stage1", "stage1b", "stage1c"]
            for g, w in enumerate(POOL_W):
                nh = w - 1
                hist = stage_all[r, hoff:hoff + nh * 128].rearrange("p (r c) -> p r c", c=128)
                hoff += nh * 128
                cs_ = slice(g * 128, (g + 1) * 128)
                D(hist, spool[:, 15 - nh:15, g * 128:(g + 1) * 128], [], ["hist%d" % g] + stk)
                V(lambda e, hist=hist, cs_=cs_: e.tensor_reduce(out=tot[r, cs_], in_=hist.rearrange("p r c -> p c r"), axis=AX.X, op=ALU.add),
                  ["hist%d" % g], ["F1"])
                V(lambda e, cs_=cs_: e.tensor_tensor(out=tot[r, cs_], in0=tot[r, cs_], in1=pin_s[r, cs_], op=ALU.add), ["F1", "pin0"], ["F1"])
                V(lambda e, cs_=cs_, w=w: e.scalar_tensor_tensor(out=tot[r, cs_], in0=tot[r, cs_], scalar=1.0 / w, in1=pin_s[r, cs_],
                                                                 op0=ALU.mult, op1=ALU.subtract), ["F1", "pin0"], ["F1"])
                T(lambda e, g=g, cs_=cs_: e.transpose(out=psQ[:, 0, g * 128:g * 128 + n], in_=tot[r, cs_], identity=ident_f[0:n, 0:n]),
                  ["F1", "ident_f"], ["q0"])
            A(lambda e: e.copy(out=rTb[:, :, 0:n], in_=psQ[:, 0, :].rearrange("p (g t) -> p g t", t=128)[:, :, 0:n]), ["q0"], ["rTb"])
            D(pools[:, 0:14, :], spool[:, 1:15, :], [], ["o_pools"])
            D(pools[:, 14, :], pin_s[r, :], ["pin0"], ["o_pools2"])
            out_keys.extend(["o_pools", "o_pools2"])
            if sub < 7:
                return
            back(n, x1s[NT * 128:NT * 128 + n, :], xt, "xt", 0)
            if debug:
                dump("atts", att[r, :], [n, 1024], ["att"])
                dump("x1s", F3[r, :], [n, 1024], ["F3"])

        if stop >= 3:
            sample_tile()
            if ntiles == NT:
                out_keys.extend(["o_kwp", "o_vwp", "o_poolp"])

        P.barrier()
        A1.off = A1.mark
        M2 = A1.alloc([3, 1024])
        wq_b = A1.alloc([8, 2048], BF16)
        skT = A1.alloc([16, 128])
        qfp = A1.alloc([2048]); qT32 = A1.alloc([16, 128]); s_sb = A1.alloc([16, 128])
        mx = A1.alloc([16, 16]); mi = A1.alloc([16, 16], U32); mif = A1.alloc([16, 16])
        wk = A1.alloc([256]); cs = A1.alloc([8, 256])
        bs = A1.alloc([8, 16]); bp = A1.alloc([8, 16], U32); au = A1.alloc([8, 16], U32); bu = A1.alloc([8, 16], U32)
        af = A1.alloc([8, 16]); bf = A1.alloc([8, 16])
        oh = A1.alloc([4, 16, 16]); i0s = A1.alloc([8, 16]); i1s = A1.alloc([8, 16])
        idx = A1.alloc([128], I32); ge = A1.alloc([8, 16]); gate = A1.alloc([128])
        acc = A1.alloc([1024])
        junkb = A1.alloc([1024], BF16)
        idxT = A1.alloc([128], I32); gateT = A1.alloc([128])
        preT = A1.alloc([128]); gT = A1.alloc([128]); coefT = A1.alloc([128])
        dgb = [A1.alloc([128], BF16) for _ in range(2)]
        gs = [A1.alloc([2048], BF16) for _ in range(NSLOT)]

        for kc in range(8):
            P.dma("pool", lambda e, kc=kc: e.dma_start(out=wq_b[:, kc, :], in_=w_query[kc * 128:(kc + 1) * 128, :]), [], ["wts"])
        for j in range(16):
            sg = stage[j % 2]
            sk = "stage%d" % (j % 2)
            D(sg[:, 0:128], sub_keys[j], [], [sk])
            T(lambda e, j=j, sg=sg: e.transpose(out=psO[:, j % 2, 0:128], in_=sg[:, 0:128], identity=ident_f), [sk, "ident_f"], ["o%d" % (j % 2)])
            A(lambda e, j=j: e.copy(out=skT[:, j, :], in_=psO[:, j % 2, 0:128]), ["o%d" % (j % 2)], ["skT"])
        D(M2, bass.AP(tensor=modS.tensor, offset=modS[0:1, 3:6, :].offset, ap=[[0, 128], [1024, 3], [1, 1024]]), ["modS"], ["M2a", "M2g"])

        slot_i = [0]

        def top16(n, src, srck, width, vals, idxs, outk):
            r = slice(0, n)
            V(lambda e: e.max(out=vals[r, 0:8], in_=src), [srck], [outk + "a"])
            V(lambda e: e.max_index(out=idxs[r, 0:8], in_max=vals[r, 0:8], in_values=src), [srck, outk + "a"], [outk + "b"])
            V(lambda e: e.match_replace(out=wk[r, 0:width], in_to_replace=vals[r, 0:8], in_values=src, imm_value=-1e30), [srck, outk + "a"], ["wk"])
            V(lambda e: e.max(out=vals[r, 8:16], in_=wk[r, 0:width]), ["wk"], [outk + "c"])
            V(lambda e: e.max_index(out=idxs[r, 8:16], in_max=vals[r, 8:16], in_values=wk[r, 0:width]), ["wk", outk + "c"], [outk + "d"])

        xtp = [xt, A1.alloc([1024])]
        hbp = [hb, A1.alloc([1024], BF16)]
        idxTp = [idxT, A1.alloc([128], I32)]
        gateTp = [gateT, A1.alloc([128])]

        def stage1_steps(t, n, par, xsrc=None, pre=()):
            r = slice(0, n)
            X, XK, HB, HK = xtp[par], "xt%d" % par, hbp[par], "hb%d" % par
            IT, ITK, GT, GTK = idxTp[par], "idxT%d" % par, gateTp[par], "gateT%d" % par
            st = list(pre)
            ss, ms, rs = st_small[r, 0:1], st_small[r, 1:2], st_small[r, 2:3]
            st.append(lambda: D(X[r, :], x1s[t * 128:t * 128 + n, :] if xsrc is None else xsrc, ["x1s"], [XK]))
            st.append(lambda: A(lambda e: e.activation(out=F1[r, :], in_=X[r, :], func=AF.Square, accum_out=ss), [XK], ["F1", "ss"]))
            st.append(lambda: V(lambda e: e.tensor_scalar(out=ms, in0=ss, scalar1=1.0 / 1024, scalar2=EPS, op0=ALU.mult, op1=ALU.add), ["ss"], ["ms"]))
            st.append(lambda: A(lambda e: e.activation(out=ms, in_=ms, func=AF.Sqrt), ["ms"], ["ms"]))
            st.append(lambda: V(lambda e: e.reciprocal(out=rs, in_=ms), ["ms"], ["rs"]))
            st.append(lambda: V(lambda e: e.scalar_tensor_tensor(out=F2[r, :], in0=X[r, :], scalar=rs, in1=M2[r, 1, :], op0=ALU.mult, op1=ALU.mult),
                                [XK, "rs", "M2a"], ["F2"]))
            st.append(lambda: V(lambda e: e.tensor_tensor(out=HB[r, :], in0=F2[r, :], in1=M2[r, 0, :], op=ALU.add), ["F2", "M2a"], [HK]))
            st.append(lambda: transpose8(n, HB, HK, hT, "hT"))
            for c4 in range(4):
                def f(c4=c4):
                    for kc in range(8):
                        T(lambda e, kc=kc: e.matmul(psD[r, :], lhsT=hT[:, kc, 0:n], rhs=wq_b[:, kc, c4 * 512:(c4 + 1) * 512],
                                                    start=(kc == 0), stop=(kc == 7)), ["hT", "wts"], ["d"])
                    A(lambda e: e.copy(out=qfp[r, c4 * 512:(c4 + 1) * 512], in_=psD[r, :]), ["d"], ["qfp"])
                st.append(f)
            for r4 in range(4):
                def f(r4=r4):
                    for jj in range(4):
                        j = r4 * 4 + jj
                        T(lambda e, j=j, jj=jj: e.transpose(out=psD[:, jj * 128:jj * 128 + n], in_=qfp[r, j * 128:(j + 1) * 128],
                                                            identity=ident_f[0:n, 0:n]), ["qfp", "ident_f"], ["d"])
                    A(lambda e: e.copy(out=qT32[:, r4 * 4:(r4 + 1) * 4, 0:n], in_=psD[:, :].rearrange("p (j t) -> p j t", t=128)[:, :, 0:n]),
                      ["d"], ["qT32"])
                st.append(f)
            for r4 in range(4):
                def f(r4=r4):
                    for jj in range(4):
                        j = r4 * 4 + jj
                        T(lambda e, j=j, jj=jj: e.matmul(psD[r, jj * 128:(jj + 1) * 128], lhsT=qT32[:, j, 0:n], rhs=skT[:, j, :],
                                                         start=True, stop=True), ["qT32", "skT"], ["d"])
                    A(lambda e: e.copy(out=s_sb[r, r4 * 4:(r4 + 1) * 4, :].rearrange("p j k -> p (j k)"), in_=psD[r, :]), ["d"], [("s_sb", r4)])
                st.append(f)
            for j in range(16):
                st.append(lambda j=j: top16(n, s_sb[r, j, :], ("s_sb", j // 4), 128, mx[:, j, :], mi[:, j, :], "tk%d" % j))
            tka = ["tk%d%s" % (j, c) for j in range(16) for c in "ac"]
            tkb = ["tk%d%s" % (j, c) for j in range(16) for c in "bd"]
            mxv = mx[r].rearrange("p (h two) k -> p h two k", two=2)
            st.append(lambda: V(lambda e: e.tensor_tensor(out=cs[r].rearrange("p h (a b) -> p h a b", b=16),
                                                          in0=mxv[:, :, 0, :].unsqueeze(3).to_broadcast([n, 8, 16, 16]),
                                                          in1=mxv[:, :, 1, :].unsqueeze(2).to_broadcast([n, 8, 16, 16]), op=ALU.add), tka, ["cs"]))
            for h in range(8):
                st.append(lambda h=h: top16(n, cs[r, h, :], "cs", 256, bs[:, h, :], bp[:, h, :], "ck%d" % h))
            cka = ["ck%d%s" % (h, c) for h in range(8) for c in "ac"]
            ckb = ["ck%d%s" % (h, c) for h in range(8) for c in "bd"]
            st.append(lambda: V(lambda e: e.tensor_single_scalar(out=au[r], in_=bp[r], scalar=4, op=ALU.logical_shift_right), ckb, ["au"]))
            st.append(lambda: V(lambda e: e.tensor_single_scalar(out=bu[r], in_=bp[r], scalar=15, op=ALU.bitwise_and), ckb, ["bu"]))
            st.append(lambda: V(lambda e: e.tensor_copy(out=af[r], in_=au[r]), ["au"], ["af"]))
            st.append(lambda: V(lambda e: e.tensor_copy(out=bf[r], in_=bu[r]), ["bu"], ["bf"]))
            st.append(lambda: V(lambda e: e.tensor_copy(out=mif[r], in_=mi[r]), tkb, ["mif"]))
            mifv = mif[r].rearrange("p (h two) k -> p h two k", two=2)
            for (sel_, selk, which, dst, dstk) in ((af, "af", 0, i0s, "i0s"), (bf, "bf", 1, i1s, "i1s")):
                for hq in range(2):
                    hs = slice(hq * 4, hq * 4 + 4)
                    st.append(lambda sel_=sel_, selk=selk, hs=hs: V(lambda e: e.tensor_tensor(
                        out=oh[r], in0=sel_[r, hs, :].unsqueeze(3).to_broadcast([n, 4, 16, 16]),
                        in1=iota16[r, :].unsqueeze(1).unsqueeze(1).to_broadcast([n, 4, 16, 16]), op=ALU.is_equal), [selk, "iota16", "oh"], ["oh"]))
                    st.append(lambda hs=hs, which=which: V(lambda e: e.tensor_tensor(
                        out=oh[r], in0=oh[r], in1=mifv[:, hs, which, :].unsqueeze(2).to_broadcast([n, 4, 16, 16]), op=ALU.mult), ["oh", "mif"], ["oh"]))
                    st.append(lambda dst=dst, dstk=dstk, hs=hs: V(lambda e: e.tensor_reduce(out=dst[r, hs, :], in_=oh[r], axis=AX.X, op=ALU.add),
                                                                 ["oh"], [dstk]))
            st.append(lambda: V(lambda e: e.scalar_tensor_tensor(out=IT[r, :], in0=i0s[r].rearrange("p h k -> p (h k)"), scalar=128.0,
                                                                 in1=i1s[r].rearrange("p h k -> p (h k)"), op0=ALU.mult, op1=ALU.add),
                                ["i0s", "i1s"], [ITK]))
            zz = st_small[r, 48:56]
            st.append(lambda: V(lambda e: e.tensor_tensor(out=ge[r], in0=bs[r], in1=bs[r, :, 0:1].to_broadcast([n, 8, 16]), op=ALU.subtract), cka, ["ge"]))
            st.append(lambda: A(lambda e: e.activation(out=ge[r], in_=ge[r], func=AF.Exp), ["ge"], ["ge"]))
            st.append(lambda: V(lambda e: e.tensor_reduce(out=zz, in_=ge[r], axis=AX.X, op=ALU.add), ["ge"], ["zz"]))
            st.append(lambda: V(lambda e: e.reciprocal(out=zz, in_=zz), ["zz"], ["zz"]))
            st.append(lambda: V(lambda e: e.tensor_tensor(out=GT[r, :].rearrange("p (h k) -> p h k", k=16), in0=ge[r],
                                                          in1=zz.unsqueeze(2).to_broadcast([n, 8, 16]), op=ALU.mult), ["ge", "zz"], [GTK]))
            if debug and t == 0:
                st.append(lambda: dump("idx0", IT, [128, 128], [ITK], I32))
                st.append(lambda: dump("gate0", GT, [128, 128], [GTK]))
            rec[0] = []
            for f_ in st:
                f_()
            flat, rec[0] = list(rec[0]), None
            return flat

        def stage2(t, n, par, y_dst, okey, filler):
            r = slice(0, n)
            X, XK, HB, HK = xtp[par], "xt%d" % par, hbp[par], "hb%d" % par
            IT, ITK, GT, GTK = idxTp[par], "idxT%d" % par, gateTp[par], "gateT%d" % par
            slots = {}

            def stA(ex):
                sl = slot_i[0] % NSLOT
                slot_i[0] += 1
                slots[ex] = sl
                P.dma("pool", lambda e: e.indirect_dma_start(
                    out=gs[sl][r, :], out_offset=None, in_=uvb, in_offset=bass.IndirectOffsetOnAxis(ap=IT[r, ex:ex + 1], axis=0)),
                    [ITK] + tabs_keys, ["gs%d" % sl])

            def stB(ex):
                sl = slots[ex]
                V(lambda e: e.scalar_tensor_tensor(
                    out=junkb[r, :], in0=gs[sl][r, 0:1024], scalar=1.0, in1=HB[r, :],
                    op0=ALU.mult, op1=ALU.mult, accum_out=preT[r, ex:ex + 1]), ["gs%d" % sl, HK], [("preT", ex), "junkb"])

            def stC1(ex):
                A(lambda e: e.activation(out=gT[r, ex:ex + 1], in_=preT[r, ex:ex + 1], func=AF.Gelu), [("preT", ex)], [("gT", ex)])
                A(lambda e: e.activation(out=coefT[r, ex:ex + 1], in_=gT[r, ex:ex + 1], func=AF.Copy, scale=GT[r, ex:ex + 1]),
                  [("gT", ex), GTK], [("coefT", ex)])

            def stC2(ex):
                sl = slots[ex]
                b2 = ex % 2
                A(lambda e: e.activation(out=dgb[b2][r, 0:n], in_=ident_b[r, 0:n], func=AF.Copy, scale=coefT[r, ex:ex + 1]),
                  [("coefT", ex), "ident_b"], ["dg%d" % b2])
                for half in range(2):
                    T(lambda e, half=half: e.matmul(
                        psO[r, half, :], lhsT=dgb[b2][r, 0:n], rhs=gs[sl][r, 1024 + half * 512:1024 + (half + 1) * 512],
                        start=(ex == 0), stop=(ex == 127)), ["dg%d" % b2, "gs%d" % sl], ["o0", "o1"])

            fi = 0
            dve_done = 0
            ndve = sum(1 for e_, _ in filler if e_ == "dve") if filler else 0
            per = (len(filler) + 117) // 118 if filler else 0
            for i in range(128 + 4):
                if i < 128:
                    stA(i)
                if 0 <= i - 1 < 128:
                    stB(i - 1)
                if 0 <= i - 2 < 128:
                    stC1(i - 2)
                if 0 <= i - 4 < 128:
                    stC2(i - 4)
                if filler and i >= 4:
                    target = min(ndve, -(-(i - 3) * ndve // (FILL_END - 3)))
                    while fi < len(filler):
                        if filler[fi][0] == "dve":
                            if dve_done >= target:
                                break
                            dve_done += 1
                        filler[fi][1]()
                        fi += 1
            while filler and fi < len(filler):
                filler[fi][1]()
                fi += 1
            V(lambda e: e.tensor_tensor(out=acc[r, :], in0=psO[r, :, :].rearrange("p a c -> p (a c)"), in1=M2[r, 2, :], op=ALU.mult),
              ["o0", "o1", "M2g"], ["acc"])
            V(lambda e: e.tensor_tensor(out=acc[r, :], in0=acc[r, :], in1=X[r, :], op=ALU.add), ["acc", XK], ["acc"])
            D(y_dst, acc[r, :], ["acc"], [okey])
            out_keys.append(okey)

        maskg = A0.alloc([8]); esel_b = A0.alloc([16], BF16)
        D(maskg, c_maskg, [], ["maskg"])
        D(st_small[:, 8:24], c_esel, [], ["hr"])
        V(lambda e: e.tensor_copy(out=esel_b, in_=st_small[:, 8:24]), ["hr"], ["esel_b"])
        its = A1.alloc([16], I32); gts = A1.alloc([16]); selt = A1.alloc([128]); selr = A1.alloc([16])

        def stage2_sample(par, y_dst, okey):
            X, XK, HB, HK = xtp[par], "xt%d" % par, hbp[par], "hb%d" % par
            IT, ITK, GT, GTK = idxTp[par], "idxT%d" % par, gateTp[par], "gateT%d" % par
            for (src, srck, dst, dstk) in ((IT, ITK, its, "its"), (GT, GTK, gts, "gts")):
                V(lambda e, src=src: e.tensor_copy(out=selt, in_=src), [srck], ["selt"])
                V(lambda e: e.tensor_tensor(out=selt.rearrange("p (g j) -> p g j", j=16), in0=selt.rearrange("p (g j) -> p g j", j=16),
                                            in1=maskg.unsqueeze(2).to_broadcast([128, 8, 16]), op=ALU.mult), ["selt", "maskg"], ["selt"])
                V(lambda e: e.tensor_reduce(out=selr, in_=selt.rearrange("p (g j) -> p j g", j=16), axis=AX.X, op=ALU.add),
                  ["selt"], ["selr"])
                V(lambda e, dst=dst: e.tensor_copy(out=dst, in_=selr), ["selr"], [dstk])
            slots = {}
            for j in range(16 + 4):
                if j < 16:
                    sl = slot_i[0] % NSLOT
                    slot_i[0] += 1
                    slots[j] = sl
                    P.dma("pool", lambda e, sl=sl, j=j: e.indirect_dma_start(
                        out=gs[sl][:, :], out_offset=None, in_=uvb, in_offset=bass.IndirectOffsetOnAxis(ap=its[:, j:j + 1], axis=0)),
                        ["its"] + tabs_keys, ["gs%d" % sl])
                if 0 <= j - 1 < 16:
                    ex = j - 1
                    sl = slots[ex]
                    V(lambda e, sl=sl, ex=ex: e.scalar_tensor_tensor(
                        out=junkb[:, :], in0=gs[sl][:, 0:1024], scalar=1.0, in1=HB[:, :],
                        op0=ALU.mult, op1=ALU.mult, accum_out=preT[:, ex:ex + 1]), ["gs%d" % sl, HK], [("preT", ex), "junkb"])
                if 0 <= j - 2 < 16:
                    ex = j - 2
                    A(lambda e, ex=ex: e.activation(out=gT[:, ex:ex + 1], in_=preT[:, ex:ex + 1], func=AF.Gelu), [("preT", ex)], [("gT", ex)])
                    A(lambda e, ex=ex: e.activation(out=coefT[:, ex:ex + 1], in_=gT[:, ex:ex + 1], func=AF.Copy, scale=gts[:, ex:ex + 1]),
                      [("gT", ex), "gts"], [("coefT", ex)])
                if 0 <= j - 4 < 16:
                    ex = j - 4
                    sl = slots[ex]
                    b2 = ex % 2
                    A(lambda e, ex=ex, b2=b2: e.activation(out=dgb[b2][:, 0:NS], in_=esel_b, func=AF.Copy, scale=coefT[:, ex:ex + 1]),
                      [("coefT", ex), "esel_b"], ["dg%d" % b2])
                    for half in range(2):
                        T(lambda e, half=half, sl=sl, b2=b2, ex=ex: e.matmul(
                            psO[0:NS, half, :], lhsT=dgb[b2][:, 0:NS], rhs=gs[sl][:, 1024 + half * 512:1024 + (half + 1) * 512],
                            start=(ex == 0), stop=(ex == 15)), ["dg%d" % b2, "gs%d" % sl], ["o0", "o1"])
            r = slice(0, NS)
            V(lambda e: e.tensor_tensor(out=acc[r, :], in0=psO[r, :, :].rearrange("p a c -> p (a c)"), in1=M2[r, 2, :], op=ALU.mult),
              ["o0", "o1", "M2g"], ["acc"])
            V(lambda e: e.tensor_tensor(out=acc[r, :], in0=acc[r, :], in1=X[r, :], op=ALU.add), ["acc", XK], ["acc"])
            D(y_dst, acc[r, :], ["acc"], [okey])
            out_keys.append(okey)

        npt = ntiles if stop >= 4 else 0
        if npt:
            for f_ in stage1_steps(0, 128, 0):
                f_[1]()
        xrep = bass.AP(tensor=x1s.tensor, offset=x1s[NT * 128:NT * 128 + 1, :].offset, ap=[[0, 8], [1024, NS], [1, 1024]])
        m2a_rep = bass.AP(tensor=modS.tensor, offset=modS[1:2, 3:5, :].offset, ap=[[0, 8], [6 * 1024, NS], [1024, 2], [1, 1024]])
        m2g_rep = bass.AP(tensor=modS.tensor, offset=modS[1:2, 5:6, :].offset, ap=[[0, 8], [6 * 1024, NS], [1, 1024]])
        s_pre = [lambda: D(M2[:, 0:2, :], m2a_rep, ["modS"], ["M2a"])]
        do_s = stop >= 5 and npt == NT
        for t in range(npt):
            if t + 1 < npt:
                nxt = stage1_steps(t + 1, 128, (t + 1) % 2)
            elif do_s:
                nxt = stage1_steps(NT, 128, NT % 2, xsrc=xrep, pre=s_pre)
            else:
                nxt = None
            stage2(t, 128, t % 2, yp[t * 128:(t + 1) * 128, :], "o_yp%d" % t, nxt)
        if stop >= 5:
            par = NT % 2
            if not do_s:
                for f_ in stage1_steps(NT, 128, par, xsrc=xrep, pre=s_pre):
                    f_[1]()
            D(M2[:, 2, :], m2g_rep, ["modS"], ["M2g"])
            stage2_sample(par, ys, "o_ys")

        P.op("sp", lambda e: e.nop(), out_keys, [])
        P.emit()
    return nc


_NC_CACHE = {}


def _in_maps(inp):
    ident, R, Mp, iota, maskg, esel = _consts()
    f = lambda a: np.ascontiguousarray(np.asarray(a, dtype=np.float32))
    shared = {
        "table": f(inp["rel_bias_table"]), "ada_w": f(inp["ada_w"][0]), "ada_b": f(inp["ada_b"]), "g1": f(inp["norm1_g"]),
        "w_in": f(inp["w_in"][0]), "gq": f(inp["q_norm_g"]), "gk": f(inp["k_norm_g"]), "sinks": f(inp["attn_sinks"]),
        "pool_w": f(inp["pool_w"][0]), "pool_scale": f(inp["pool_scale"]), "w_out": f(inp["w_out"][0]), "g2": f(inp["norm2_g"]),
        "w_query": f(inp["peer_w_query"][0]), "sub_keys": f(inp["peer_sub_keys"][0]).reshape(16, 128, 128),
        "peer_u": f(inp["peer_u"][0]), "peer_v": f(inp["peer_v"][0]),
        "c_ident": ident, "c_R": R, "c_Mp": Mp, "c_iota": iota, "c_maskg": maskg, "c_esel": esel,
    }
    maps = []
    for c in range(8):
        sl = slice(c * NS, (c + 1) * NS)
        m = dict(shared)
        m["xp"] = f(inp["x_prompt"][c])
        m["xs"] = f(inp["x_sample"][sl, 0, :])
        m["ck"] = f(inp["cache_k_win"][0, sl]).reshape(NS, 128, 128)
        m["cv"] = f(inp["cache_v_win"][0, sl]).reshape(NS, 128, 128)
        m["spool"] = f(inp["state_pool"][0, sl])
        m["c17"] = np.concatenate([f(inp["c_prompt"][c:c + 1]), f(inp["c_sample"][sl])], axis=0)
        maps.append(m)
    return maps


def kernel(**inputs):
    if "nc" not in _NC_CACHE:
        _NC_CACHE["nc"] = build_nc(False)
    nc = _NC_CACHE["nc"]
    res = run_bass_kernel_spmd(nc, _in_maps(inputs), core_ids=list(range(8)))
    R = res.results
    y_prompt = np.stack([R[c]["yp"].reshape(2048, 1024) for c in range(8)], 0)
    y_sample = np.concatenate([R[c]["ys"] for c in range(8)], 0).reshape(128, 1, 1024)
    kwp = np.stack([R[c]["kwp"].reshape(128, 2, 64) for c in range(8)], 0)[None]
    vwp = np.stack([R[c]["vwp"].reshape(128, 2, 64) for c in range(8)], 0)[None]
    poolp = np.stack([R[c]["poolp"] for c in range(8)], 0)[None]
    kws = np.concatenate([R[c]["kws"].reshape(NS, 128, 2, 64) for c in range(8)], 0)[None]
    vws = np.concatenate([R[c]["vws"].reshape(NS, 128, 2, 64) for c in range(8)], 0)[None]
    pools = np.concatenate([R[c]["pools"] for c in range(8)], 0)[None]
    return tuple(np.ascontiguousarray(a, dtype=np.float32) for a in (y_prompt, y_sample, kwp, vwp, poolp, kws, vws, pools))
```

```python
import contextlib
import math
import numpy as np
import concourse.bass as bass
import concourse.mybir as mybir
from concourse.bass_utils import run_bass_kernel_spmd

F32 = mybir.dt.float32
BF16 = mybir.dt.bfloat16
I32 = mybir.dt.int32
U32 = mybir.dt.uint32
ALU = mybir.AluOpType
AF = mybir.ActivationFunctionType
AX = mybir.AxisListType

ENGS = ("pe", "dve", "act", "pool", "sp")
EPS = 1e-6
NT = 16
NS = 16
NSLOT = 13
POOL_W = (2, 4, 8, 16)
FILL_END = 125


class Op:
    __slots__ = ("eng", "fn", "deps", "is_dma", "inc_val", "dsem", "dval", "has_dep", "dprev", "line")

    def __init__(self, eng, fn, is_dma):
        self.eng = eng
        self.fn = fn
        self.is_dma = is_dma
        self.deps = []
        self.inc_val = None
        self.dsem = None
        self.dval = None
        self.dprev = None
        self.has_dep = False


class Prog:
    N_DMA_SEMS = 28
    N_HW_SEMS = 14
    PSUM_KEYS = frozenset(["q0", "q1", "o", "o0", "o1", "d", "t"])

    def __init__(self, nc):
        self.nc = nc
        self.ops = {e: [] for e in ENGS}
        self.last_w = {}
        self.readers = {}
        self.dma_i = 0
        self.dma_sw_i = 0
        self.dma_uses = [0] * self.N_DMA_SEMS
        self.last_dma_on_sem = [None] * self.N_DMA_SEMS

    def _add(self, eng, fn, reads, writes, is_dma, extra_deps=()):
        op = Op(eng, fn, is_dma)
        op.line = fn.__code__.co_firstlineno
        deps = set(extra_deps)
        xr = [k for k in reads if k in self.PSUM_KEYS]
        if xr:
            reads = [k for k in reads if k not in self.PSUM_KEYS]
            writes = list(writes) + xr
        for k in list(reads) + list(writes):
            w = self.last_w.get(k)
            if w is not None:
                deps.add(w)
        for k in writes:
            for r in self.readers.get(k, ()):
                deps.add(r)
        deps.discard(op)
        if eng == "pe" and not is_dma:
            deps = {d for d in deps if not (d.eng == "pe" and not d.is_dma)}
        op.deps = list(deps)
        for d in op.deps:
            d.has_dep = True
        for k in writes:
            self.last_w[k] = op
            self.readers[k] = []
        for k in reads:
            self.readers.setdefault(k, []).append(op)
        if is_dma:
            if eng == "pool":
                j = self.N_HW_SEMS + self.dma_sw_i % (self.N_DMA_SEMS - self.N_HW_SEMS)
                self.dma_sw_i += 1
            else:
                j = self.dma_i % self.N_HW_SEMS
                self.dma_i += 1
            op.dsem = j
            op.dprev = 16 * self.dma_uses[j]
            self.dma_uses[j] += 1
            op.dval = 16 * self.dma_uses[j]
            self.last_dma_on_sem[j] = op
        self.ops[eng].append(op)
        return op

    def op(self, eng, fn, reads=(), writes=()):
        return self._add(eng, fn, reads, writes, False)

    def dma(self, eng, fn, reads=(), writes=()):
        return self._add(eng, fn, reads, writes, True)

    def barrier(self):
        tails = []
        for e in ENGS:
            for o in reversed(self.ops[e]):
                if not o.is_dma:
                    tails.append(o)
                    break
        dmas = [o for o in self.last_dma_on_sem if o is not None]
        for e in ENGS:
            self._add(e, lambda eng: eng.nop(), (), (), False, extra_deps=tails + dmas)

    def emit(self):
        nc = self.nc
        for e in ENGS:
            c = 0
            for op in self.ops[e]:
                if not op.is_dma and op.has_dep:
                    c += 1
                    op.inc_val = c
        with contextlib.ExitStack() as st:
            esem = {e: st.enter_context(nc.semaphore("s_" + e)) for e in ENGS}
            dsems = [st.enter_context(nc.semaphore("d%d" % i)) for i in range(self.N_DMA_SEMS)]
            block = st.enter_context(nc.Block())
            prog = self

            def run(e, eng):
                waited = {}
                for op in prog.ops[e]:
                    need = {}
                    for d in op.deps:
                        if d.is_dma:
                            key = ("d", d.dsem)
                            v = d.dval
                        else:
                            key = ("e", d.eng)
                            v = d.inc_val
                        if need.get(key, 0) < v:
                            need[key] = v
                    if op.is_dma and op.dprev > 0:
                        key = ("d", op.dsem)
                        if need.get(key, 0) < op.dprev:
                            need[key] = op.dprev
                    for key, v in need.items():
                        if waited.get(key, 0) >= v:
                            continue
                        waited[key] = v
                        sem = dsems[key[1]] if key[0] == "d" else esem[key[1]]
                        eng.wait_ge(sem, v)
                    ins = op.fn(eng)
                    if op.is_dma:
                        ins.then_inc(dsems[op.dsem], 16)
                    elif op.inc_val is not None:
                        ins.then_inc(esem[e], 1)

            @block.tensor
            def _(eng):
                run("pe", eng)

            @block.vector
            def _(eng):
                run("dve", eng)

            @block.scalar
            def _(eng):
                run("act", eng)

            @block.gpsimd
            def _(eng):
                run("pool", eng)

            @block.sync
            def _(eng):
                run("sp", eng)


class Arena:
    def __init__(self, t, nwords):
        self.t = t
        self.n = nwords
        self.off = 0
        self.mark = 0

    def alloc(self, shape, dt=F32):
        n = 1
        for s in shape:
            n *= s
        words = n if dt in (F32, I32, U32) else (n + 1) // 2
        words = (words + 15) // 16 * 16
        assert self.off + words <= self.n, ("arena overflow", self.off, words, self.n)
        v = self.t[:, self.off:self.off + words]
        self.off += words
        if dt != F32:
            v = v.bitcast(dt)
        v = v[:, 0:n]
        if len(shape) == 1:
            return v
        names = " ".join("d%d" % i for i in range(len(shape)))
        kw = {"d%d" % i: shape[i] for i in range(len(shape))}
        return v.rearrange("p (%s) -> p %s" % (names, names), **kw)


def _consts():
    import jax
    import jax.numpy as jnp
    ident = np.eye(128, dtype=np.float32)
    with jax.default_device(jax.devices("cpu")[0]):
        d = jnp.arange(128)
        df = jnp.maximum(d, 1).astype(jnp.float32)
        large = 16 + (jnp.log(df / 16) / math.log(128 / 16) * 16).astype(jnp.int32)
        large = jnp.minimum(large, 31)
        bucket = np.asarray(jnp.where(d < 16, d, large))
    R = np.zeros((33, 384), np.float32)
    for j in range(384):
        m = 383 - j
        if 128 <= m < 256:
            R[bucket[m - 128], j] = 1.0
        else:
            R[32, j] = 1.0
    Mp = np.zeros((128, 3, 4, 128), np.float32)
    for g, w in enumerate(POOL_W):
        for t in range(128):
            for tp in range(max(0, t - w + 1), t + 1):
                Mp[tp, 0, g, t] += 1.0 / w
                Mp[tp, 1, g, t] += 1.0 / min(w, t + 1)
            Mp[t, 0, g, t] -= 1.0
            Mp[t, 1, g, t] -= 1.0
            for tp in range(128):
                if tp > 128 + t - w:
                    Mp[tp, 2, g, t] = 1.0 / w
    iota = np.tile(np.arange(16, dtype=np.float32)[None, :], (128, 1))
    p = np.arange(128)
    maskg = (p[:, None] // 16 == np.arange(8)[None, :]).astype(np.float32)
    esel = (p[:, None] % 16 == np.arange(16)[None, :]).astype(np.float32)
    return ident, R, Mp.reshape(128, 3 * 4 * 128), iota, maskg, esel


def build_nc(debug=False, stop=99, ntiles=NT, sub=99):
    nc = bass.Bass("TRN2", target_bir_lowering=False)

    def din(name, shape, dt=F32):
        return nc.dram_tensor(name, list(shape), dt, kind="ExternalInput").ap()

    def dout(name, shape, dt=F32):
        return nc.dram_tensor(name, list(shape), dt, kind="ExternalOutput").ap()

    xp = din("xp", [NT * 128, 1024]); xs = din("xs", [NS, 1024])
    ck = din("ck", [NS, 128, 128]); cv = din("cv", [NS, 128, 128]); spool = din("spool", [NS, 15, 512])
    c17 = din("c17", [17, 1024]); table = din("table", [32, 16])
    ada_w = din("ada_w", [1024, 6144]); ada_b = din("ada_b", [1, 6144]); g1 = din("g1", [1, 1024])
    w_in = din("w_in", [1024, 3840]); gq = din("gq", [1, 64]); gk = din("gk", [1, 64]); sinks = din("sinks", [1, 16])
    pool_w = din("pool_w", [4, 128, 256]); pool_scale = din("pool_scale", [1, 1024]); w_out = din("w_out", [1024, 1024])
    g2 = din("g2", [1, 1024]); w_query = din("w_query", [1024, 2048]); sub_keys = din("sub_keys", [16, 128, 128])
    peer_u = din("peer_u", [16384, 1024]); peer_v = din("peer_v", [16384, 1024])
    c_ident = din("c_ident", [128, 128]); c_R = din("c_R", [33, 384]); c_Mp = din("c_Mp", [128, 1536]); c_iota = din("c_iota", [128, 16])
    c_maskg = din("c_maskg", [128, 8]); c_esel = din("c_esel", [128, 16])

    yp = dout("yp", [NT * 128, 1024]); ys = dout("ys", [NS, 1024])
    kwp = dout("kwp", [128, 128]); vwp = dout("vwp", [128, 128]); poolp = dout("poolp", [15, 512])
    kws = dout("kws", [NS, 128, 128]); vws = dout("vws", [NS, 128, 128]); pools = dout("pools", [NS, 15, 512])
    out_keys = []

    modS = nc.dram_tensor("modS", [17, 6, 1024], F32, kind="Internal").ap()
    x1s = nc.dram_tensor("x1s", [(NT + 1) * 128, 1024], F32, kind="Internal").ap()
    krow = nc.dram_tensor("krow", [2, NS, 128], F32, kind="Internal").ap()
    uvb = nc.dram_tensor("uvb", [16384, 2048], BF16, kind="Internal").ap()

    P = Prog(nc)
    global _LASTP
    _LASTP = P
    dbg = {}

    rec = [None]

    def _emit(eng, fn, r, w, dma=False):
        r, w = list(r), list(w)
        if rec[0] is not None:
            rec[0].append((eng if not dma else "dma", lambda: (P.dma if dma else P.op)(eng, fn, r, w)))
            return None
        return (P.dma if dma else P.op)(eng, fn, r, w)

    def V(fn, r=(), w=()):
        return _emit("dve", fn, r, w)

    def A(fn, r=(), w=()):
        return _emit("act", fn, r, w)

    def T(fn, r=(), w=()):
        return _emit("pe", fn, r, w)

    def G(fn, r=(), w=()):
        return _emit("pool", fn, r, w)

    def D(out, in_, r=(), w=(), q="sp"):
        return _emit(q, lambda e: e.dma_start(out=out, in_=in_), r, w, dma=True)

    def bcast_rows(ap2d, n):
        return bass.AP(tensor=ap2d.tensor, offset=ap2d.offset, ap=[[0, n]] + [list(x) for x in ap2d.ap[1:]])

    with contextlib.ExitStack() as st:
        A0_WORDS = 5888
        A1_WORDS = 47300
        a0t = st.enter_context(nc.sbuf_tensor("arena0", [128, A0_WORDS], F32))
        a1t = st.enter_context(nc.sbuf_tensor("arena1", [128, A1_WORDS], F32))
        psQ = st.enter_context(nc.psum_tensor("psQ", [128, 4, 512], F32))
        psO = st.enter_context(nc.psum_tensor("psO", [128, 2, 512], F32))
        psD = st.enter_context(nc.psum_tensor("psD", [128, 512], F32))
        psT = st.enter_context(nc.psum_tensor("psT", [128, 8, 128], BF16))
        A0 = Arena(a0t, A0_WORDS)
        A1 = Arena(a1t, A1_WORDS)

        def dump(name, ap, shape, reads, dt=F32):
            if not debug:
                return
            o = dout("dbg_" + name, shape, dt)
            D(o, ap, reads, ["dbg_" + name])
            out_keys.append("dbg_" + name)

        ident_f = A0.alloc([128], F32)
        ident_b = A0.alloc([128], BF16)
        ones_b = A0.alloc([8], BF16)
        ones_f = A0.alloc([32], F32)
        gqs = A0.alloc([64]); gkb = A0.alloc([64]); esink = A0.alloc([16]); iota16 = A0.alloc([16])
        xt = A0.alloc([1024]); hf = A0.alloc([1024]); hb = A0.alloc([1024], BF16); hT = A0.alloc([8, 128], BF16)
        F1 = A0.alloc([1024]); F2 = A0.alloc([1024])
        st_small = A0.alloc([64])
        vf = A0.alloc([128]); knf = A0.alloc([128])
        stage = [A1.alloc([2048]) for _ in range(2)]
        A1.mark = A1.off

        D(ident_f, c_ident, [], ["ident_f"])
        A(lambda e: e.copy(out=ident_b, in_=ident_f), ["ident_f"], ["ident_b"])
        G(lambda e: e.memset(ones_b, 1.0), [], ["ones_b"])
        G(lambda e: e.memset(ones_f, 1.0), [], ["ones_f"])
        D(gqs, bcast_rows(gq, 128), [], ["gqs"])
        V(lambda e: e.tensor_scalar(out=gqs, in0=gqs, scalar1=0.125, scalar2=0.0, op0=ALU.mult, op1=ALU.add), ["gqs"], ["gqs"])
        D(gkb, bcast_rows(gk, 128), [], ["gkb"])
        D(esink, bcast_rows(sinks, 128), [], ["esink"])
        A(lambda e: e.activation(out=esink, in_=esink, func=AF.Exp), ["esink"], ["esink"])
        D(iota16, c_iota, [], ["iota16"])

        M1 = A1.alloc([3, 1024])
        w_in_b = A1.alloc([8, 3840], BF16)
        w_out_b = A1.alloc([8, 1024], BF16)
        pw_b = A1.alloc([4, 256], BF16)
        bias = [A1.alloc([16, 128]) for _ in range(2)]
        Mp = A1.alloc([3, 4, 128])
        R_sb = A1.alloc([384])
        tab_sb = A1.alloc([16])
        qf = A1.alloc([1024]); qn = A1.alloc([8, 2, 64], BF16); qTP = [A1.alloc([8, 128], BF16) for _ in range(2)]; qT = qTP[0]
        pin = [A1.alloc([512]) for _ in range(3)]
        knb = A1.alloc([128], BF16)
        kT = [A1.alloc([128], BF16) for _ in range(3)]
        Vb = [A1.alloc([128], BF16) for _ in range(3)]
        PT = A1.alloc([2, 16, 128], BF16)
        xtB = A1.alloc([1024]); F3 = A1.alloc([1024])
        rTb = A1.alloc([4, 128], BF16)
        qTs = A1.alloc([16, 8], BF16)
        off_u = A1.off
        c17t = A1.alloc([1024]); cT = A1.alloc([8, 17], BF16); gbc = [A1.alloc([1024]) for _ in range(2)]
        mch = [A1.alloc([512]) for _ in range(2)]; abt = [A1.alloc([512], BF16) for _ in range(2)]; ones_r = A1.alloc([32], BF16)
        A1.off = off_u
        sgaP = [A1.alloc([1024], BF16) for _ in range(2)]; sgbP = [A1.alloc([1024], BF16) for _ in range(2)]
        sbuf_s = A1.alloc([1024])
        att = A1.alloc([1024]); mixb = A1.alloc([1024], BF16); mixT = A1.alloc([8, 128], BF16)
        b0f = bias[0].rearrange("p h q -> p (h q)")
        kall_b = b0f[:, 0:1024].bitcast(BF16).rearrange("p (b c) -> p b c", c=128)
        vall_b = b0f[:, 1024:2048].bitcast(BF16).rearrange("p (b c) -> p b c", c=128)
        kTs = Mp.rearrange("p a g t -> p (a g t)")[:, 0:1024].bitcast(BF16).rearrange("p (b c) -> p b c", c=128)

        D(R_sb[0:33, :], c_R, [], ["R_sb"])
        D(tab_sb[0:32, :], table, [], ["tab_sb"])
        G(lambda e: e.memset(tab_sb[32:33, :], -30000.0), [], ["tab32"])
        def build_bias():
            bi = 0
            for blk in range(2):
                for qc in range(4):
                    bank = bi % 4
                    pk = "q%d" % (bank // 2)
                    bi += 1
                    for ql in range(32):
                        q = qc * 32 + ql
                        off = (255 - q) if blk == 1 else (127 - q)
                        T(lambda e, bank=bank, ql=ql, off=off: e.matmul(psQ[:, bank, ql * 16:(ql + 1) * 16], lhsT=R_sb[0:33, off:off + 128],
                                                                       rhs=tab_sb[0:33, :], start=True, stop=True),
                          ["R_sb", "tab_sb", "tab32"], [pk])
                    dst = bias[blk][:, :, qc * 32:(qc + 1) * 32]
                    src = psQ[:, bank, :].rearrange("p (q h) -> p h q", h=16)
                    if bi % 2 == 0:
                        A(lambda e, dst=dst, src=src: e.copy(out=dst, in_=src), [pk], ["bias"])
                    else:
                        V(lambda e, dst=dst, src=src: e.tensor_copy(out=dst, in_=src), [pk], ["bias"])


        build_bias()

        for kc in range(8):
            P.dma("pool", lambda e, kc=kc: e.dma_start(out=w_in_b[:, kc, :], in_=w_in[kc * 128:(kc + 1) * 128, :]), [], ["wts"])
        for kc in range(8):
            P.dma("pool", lambda e, kc=kc: e.dma_start(out=w_out_b[:, kc, :], in_=w_out[kc * 128:(kc + 1) * 128, :]), [], ["wts"])
        D(c17t[0:17, :], c17, [], ["c17t"])
        A(lambda e: e.activation(out=c17t[0:17, :], in_=c17t[0:17, :], func=AF.Silu), ["c17t"], ["c17t"])
        G(lambda e: e.memset(ones_r, 1.0), [], ["ones_r"])
        for kc in range(8):
            T(lambda e, kc=kc: e.transpose(out=psO[:, 0, kc * 32:kc * 32 + 17], in_=c17t[0:17, kc * 128:(kc + 1) * 128],
                                           identity=ident_f[0:17, 0:17]), ["c17t", "ident_f"], ["o"])
        A(lambda e: e.copy(out=cT, in_=psO[:, 0, 0:256].rearrange("p (k c) -> p k c", c=32)[:, :, 0:17]), ["o"], ["cT"])
        D(gbc[0][0:17, :], bcast_rows(g1, 17), [], ["gbc0"])
        D(gbc[1][0:17, :], bcast_rows(g2, 17), [], ["gbc1"])
        for cc in range(12):
            sgb_ = stage[cc % 2].bitcast(BF16)
            sk = "stage%d" % (cc % 2)
            seg, co = cc // 2, (cc % 2) * 512
            P.dma("pool", lambda e, sgb_=sgb_, cc=cc: e.dma_start(
                out=sgb_.rearrange("p (k c) -> p k c", c=512),
                in_=ada_w[:, cc * 512:(cc + 1) * 512].rearrange("(k p) c -> p k c", p=128)), [], [sk])
            P.dma("pool", lambda e, cc=cc: e.dma_start(out=abt[cc % 2][0:1, :], in_=ada_b[:, cc * 512:(cc + 1) * 512]), [], ["abt%d" % (cc % 2)])
            pk = "q%d" % (cc % 2)
            pso = psQ[0:17, (cc % 2) * 2, 0:512]
            for kc in range(8):
                T(lambda e, kc=kc, sgb_=sgb_, pso=pso: e.matmul(pso, lhsT=cT[:, kc, :], rhs=sgb_[:, kc * 512:(kc + 1) * 512],
                                                                start=(kc == 0), stop=False), ["cT", sk], [pk])
            T(lambda e, pso=pso, cc=cc: e.matmul(pso, lhsT=ones_r[0:1, 0:17], rhs=abt[cc % 2][0:1, :], start=False, stop=True),
              ["ones_r", "abt%d" % (cc % 2)], [pk])
            mc = mch[cc % 2]
            mk = "mch%d" % (cc % 2)
            if seg in (1, 4):
                gb_ = gbc[0] if seg == 1 else gbc[1]
                V(lambda e, mc=mc, pso=pso, gb_=gb_, co=co: e.scalar_tensor_tensor(
                    out=mc[0:17, :], in0=pso, scalar=1.0, in1=gb_[0:17, co:co + 512], op0=ALU.add, op1=ALU.mult),
                  [pk, "gbc0", "gbc1"], [mk])
            else:
                A(lambda e, mc=mc, pso=pso: e.copy(out=mc[0:17, :], in_=pso), [pk], [mk])
            D(modS[:, seg, co:co + 512], mc[0:17, :], [mk], ["modS"])

        cnt = [0]

        def cvt(dst, src_dram, ncols):
            i = cnt[0]
            cnt[0] += 1
            sg = stage[i % 2]
            sk = "stage%d" % (i % 2)
            D(sg[:, 0:ncols], src_dram, [], [sk])
            if i % 2 == 0:
                A(lambda e: e.copy(out=dst, in_=sg[:, 0:ncols]), [sk], ["wts"])
            else:
                V(lambda e: e.tensor_copy(out=dst, in_=sg[:, 0:ncols]), [sk], ["wts"])

        D(stage[0][:, 0:1024].rearrange("p (g o) -> p g o", g=4), pool_w.rearrange("g c o -> c g o"), [], ["stage0"])
        D(stage[1][:, 0:1024], bcast_rows(pool_scale, 128), [], ["stage1"])
        V(lambda e: e.tensor_tensor(out=pw_b.rearrange("p g o -> p (g o)"), in0=stage[0][:, 0:1024], in1=stage[1][:, 0:1024], op=ALU.mult),
          ["stage0", "stage1"], ["wts"])
        D(Mp.rearrange("p a g t -> p (a g t)"), c_Mp, [], ["Mp"])

        D(M1, bass.AP(tensor=modS.tensor, offset=modS[0:1, 0:3, :].offset, ap=[[0, 128], [1024, 3], [1, 1024]]), ["modS"], ["M1"])
        P.barrier()

        for hi_, src_t in enumerate((peer_u, peer_v)):
            for k in range(8):
                rows = slice(k * 2048, (k + 1) * 2048)
                P.dma("pool", lambda e, src_t=src_t, rows=rows, hi_=hi_: e.dma_start(
                    out=uvb[rows, hi_ * 1024:(hi_ + 1) * 1024], in_=src_t[rows, :]), [], [("tabs", hi_, k)])
        tabs_keys = [("tabs", hi_, k) for hi_ in range(2) for k in range(8)]

        def rms_mod(n, M, Mk, want_f32, xt=xt, xk="xt"):
            r = slice(0, n)
            ss, ms, rs = st_small[r, 0:1], st_small[r, 1:2], st_small[r, 2:3]
            A(lambda e: e.activation(out=F1[r, :], in_=xt[r, :], func=AF.Square, accum_out=ss), [xk], ["F1", "ss"])
            V(lambda e: e.tensor_scalar(out=ms, in0=ss, scalar1=1.0 / 1024, scalar2=EPS, op0=ALU.mult, op1=ALU.add), ["ss"], ["ms"])
            A(lambda e: e.activation(out=ms, in_=ms, func=AF.Sqrt), ["ms"], ["ms"])
            V(lambda e: e.reciprocal(out=rs, in_=ms), ["ms"], ["rs"])
            V(lambda e: e.scalar_tensor_tensor(out=F2[r, :], in0=xt[r, :], scalar=rs, in1=M[r, 1, :], op0=ALU.mult, op1=ALU.mult),
              [xk, "rs", Mk], ["F2"])
            if want_f32:
                V(lambda e: e.tensor_tensor(out=hf[r, :], in0=F2[r, :], in1=M[r, 0, :], op=ALU.add), ["F2", Mk], ["hf"])
                A(lambda e: e.copy(out=hb[r, :], in_=hf[r, :]), ["hf"], ["hb"])
            else:
                V(lambda e: e.tensor_tensor(out=hb[r, :], in0=F2[r, :], in1=M[r, 0, :], op=ALU.add), ["F2", Mk], ["hb"])

        psDb = psD[:, :].bitcast(BF16).rearrange("p (k c) -> p k c", c=128)

        def transpose8(n, src, srck, dst, dstk, ps=psT, psk="t"):
            for kc in range(8):
                T(lambda e, kc=kc: e.transpose(out=ps[:, kc, 0:n], in_=src[0:n, kc * 128:(kc + 1) * 128], identity=ident_b[0:n, 0:n]),
                  [srck, "ident_b"], [psk])
            A(lambda e: e.copy(out=dst[:, :, 0:n], in_=ps[:, :, 0:n]), [psk], [dstk])

        def proj(n, c0, c1, outs, pk):
            c = c0
            for (pso, nc_) in outs:
                for kc in range(8):
                    T(lambda e, kc=kc, pso=pso, c=c, nc_=nc_: e.matmul(pso, lhsT=hT[:, kc, 0:n], rhs=w_in_b[:, kc, c:c + nc_],
                                                                       start=(kc == 0), stop=(kc == 7)), ["hT", "wts"], [pk])
                c += nc_
            assert c == c1

        def head_rstd(n, src_ps, pk, nh, col0):
            r = slice(0, n)
            sq = F1[r, 0:nh * 64]
            ssq = st_small[r, col0:col0 + nh]
            A(lambda e: e.activation(out=sq, in_=src_ps, func=AF.Square), [pk], ["F1"])
            V(lambda e: e.tensor_reduce(out=ssq, in_=sq.rearrange("p (h d) -> p h d", d=64), axis=AX.X, op=ALU.add), ["F1"], ["hr"])
            V(lambda e: e.tensor_scalar(out=ssq, in0=ssq, scalar1=1.0 / 64, scalar2=EPS, op0=ALU.mult, op1=ALU.add), ["hr"], ["hr"])
            A(lambda e: e.activation(out=ssq, in_=ssq, func=AF.Sqrt), ["hr"], ["hr"])
            V(lambda e: e.reciprocal(out=ssq, in_=ssq), ["hr"], ["hr"])
            return ssq

        def front(n, par, gi=0, X=xt, XK="xt"):
            qT, sga, sgb = qTP[gi], sgaP[gi], sgbP[gi]
            r = slice(0, n)
            rms_mod(n, M1, "M1", False, X, XK)
            transpose8(n, hb, "hb", hT, "hT")
            proj(n, 0, 1024, [(psQ[r, 0, :], 512), (psQ[r, 1, :], 512)], "q0")
            qps = psQ[r, 0:2, :].rearrange("p a (h d) -> p (a h) d", d=64)
            rq = head_rstd(n, psQ[r, 0:2, :].rearrange("p a c -> p (a c)"), "q0", 16, 4)
            V(lambda e: e.tensor_tensor(out=qf[r, :].rearrange("p (h d) -> p h d", d=64), in0=qps,
                                        in1=rq.unsqueeze(2).to_broadcast([n, 16, 64]), op=ALU.mult), ["q0", "hr"], ["qf"])
            V(lambda e: e.tensor_tensor(out=qn[r].rearrange("p hh g d -> p g hh d"),
                                        in0=qf[r, :].rearrange("p (g hh d) -> p g hh d", g=2, hh=8),
                                        in1=gqs[r, :].unsqueeze(1).unsqueeze(1).to_broadcast([n, 2, 8, 64]), op=ALU.mult),
              ["qf", "gqs"], ["qn"])
            for hh in range(8):
                T(lambda e, hh=hh: e.transpose(out=psT[:, hh, 0:n], in_=qn[r, hh, :, :].rearrange("p g d -> p (g d)"),
                                               identity=ident_b[0:n, 0:n]), ["qn", "ident_b"], ["t"])
            A(lambda e: e.copy(out=qT[:, :, 0:n], in_=psT[:, :, 0:n]), ["t"], ["qT%d" % gi])
            proj(n, 1024, 1792, [(psQ[r, 0, :], 512), (psQ[r, 1, 0:256], 256)], "q0")
            A(lambda e: e.copy(out=pin[par][r, 0:256], in_=psQ[r, 0, 256:512]), ["q0"], ["pin%d" % par])
            A(lambda e: e.copy(out=pin[par][r, 256:512], in_=psQ[r, 1, 0:256]), ["q0"], ["pin%d" % par])
            A(lambda e: e.copy(out=vf[r, :], in_=psQ[r, 0, 128:256]), ["q0"], ["vf"])
            V(lambda e: e.tensor_copy(out=Vb[par][r, :], in_=psQ[r, 0, 128:256]), ["q0"], ["Vb%d" % par])
            rk = head_rstd(n, psQ[r, 0, 0:128], "q0", 2, 24)
            V(lambda e: e.tensor_tensor(out=knf[r, :].rearrange("p (h d) -> p h d", d=64),
                                        in0=psQ[r, 0, 0:128].rearrange("p (h d) -> p h d", d=64),
                                        in1=rk.unsqueeze(2).to_broadcast([n, 2, 64]), op=ALU.mult), ["q0", "hr"], ["knf"])
            V(lambda e: e.tensor_tensor(out=knf[r, :].rearrange("p (h d) -> p h d", d=64),
                                        in0=knf[r, :].rearrange("p (h d) -> p h d", d=64),
                                        in1=gkb[r, :].unsqueeze(1).to_broadcast([n, 2, 64]), op=ALU.mult), ["knf", "gkb"], ["knf"])
            proj(n, 1792, 2816, [(psQ[r, 0, :], 512), (psQ[r, 1, :], 512)], "q0")
            A(lambda e: e.activation(out=sga[r, :], in_=psQ[r, 0:2, :].rearrange("p a c -> p (a c)"), func=AF.Sigmoid), ["q0"], ["sga%d" % gi])
            proj(n, 2816, 3840, [(psQ[r, 0, :], 512), (psQ[r, 1, :], 512)], "q0")
            A(lambda e: e.activation(out=sgb[r, :], in_=psQ[r, 0:2, :].rearrange("p a c -> p (a c)"), func=AF.Sigmoid), ["q0"], ["sgb%d" % gi])

        def attn_finalize(n, gi=0):
            r = slice(0, n)
            sga = sgaP[gi]
            den = st_small[r, 32:48]
            V(lambda e: e.tensor_tensor(out=den, in0=psD[r, 256:272], in1=esink[r, :], op=ALU.add), ["d", "esink"], ["den"])
            V(lambda e: e.reciprocal(out=den, in_=den), ["den"], ["den"])
            V(lambda e: e.tensor_tensor(out=att[r, :].rearrange("p (h d) -> p h d", d=64),
                                        in0=psO[r, :, :].rearrange("p a (h d) -> p (a h) d", d=64),
                                        in1=den.unsqueeze(2).to_broadcast([n, 16, 64]), op=ALU.mult), ["o", "den"], ["att"])
            V(lambda e: e.tensor_tensor(out=att[r, :], in0=att[r, :], in1=sga[r, :], op=ALU.mult), ["att", "sga%d" % gi], ["att"])

        def back(n, x1_dst, X=xt, XK="xt", gi=0):
            r = slice(0, n)
            F2 = F3
            sgb = sgbP[gi]
            for g in range(4):
                T(lambda e, g=g: e.matmul(psQ[r, 2 + g // 2, (g % 2) * 256:(g % 2) * 256 + 256], lhsT=rTb[:, g, 0:n], rhs=pw_b[:, g, :],
                                          start=True, stop=True), ["rTb", "wts"], ["q1"])
            V(lambda e: e.tensor_tensor(out=F2[r, :], in0=psQ[r, 2:4, :].rearrange("p a c -> p (a c)"), in1=sgb[r, :], op=ALU.mult),
              ["q1", "sgb%d" % gi], ["F3"])
            V(lambda e: e.tensor_tensor(out=mixb[r, :], in0=F2[r, :], in1=att[r, :], op=ALU.add), ["F3", "att"], ["mixb"])
            transpose8(n, mixb, "mixb", mixT, "mixT", psDb, "d")
            for half in range(2):
                for kc in range(8):
                    T(lambda e, half=half, kc=kc: e.matmul(psQ[r, 2 + half, :], lhsT=mixT[:, kc, 0:n], rhs=w_out_b[:, kc, half * 512:(half + 1) * 512],
                                                           start=(kc == 0), stop=(kc == 7)), ["mixT", "wts"], ["q1"])
            V(lambda e: e.tensor_tensor(out=F2[r, :], in0=psQ[r, 2:4, :].rearrange("p a c -> p (a c)"), in1=M1[r, 2, :], op=ALU.mult),
              ["q1", "M1"], ["F3"])
            V(lambda e: e.tensor_tensor(out=F2[r, :], in0=F2[r, :], in1=X[r, :], op=ALU.add), ["F3", XK], ["F3"])
            D(x1_dst, F2[r, :], ["F3"], ["x1s"])

        def prompt_front(t):
            par = t % 3
            gi = t % 2
            X, XK = (xt, "xt") if gi == 0 else (xtB, "xtB")
            D(X, xp[t * 128:(t + 1) * 128, :], [], [XK])
            front(128, par, gi, X, XK)
            A(lambda e: e.copy(out=knb, in_=knf), ["knf"], ["knb"])
            T(lambda e: e.transpose(out=psT[:, 0, :], in_=knb, identity=ident_b), ["knb", "ident_b"], ["t"])
            A(lambda e: e.copy(out=kT[par], in_=psT[:, 0, :]), ["t"], ["kT%d" % par])
            if t == NT - 1:
                D(kwp, knf, ["knf"], ["o_kwp"]); D(vwp, vf, ["vf"], ["o_vwp"])
                D(poolp, pin[par][113:128, :], ["pin%d" % par], ["o_poolp"])

        def prompt_back(t):
            par, prv = t % 3, (t - 1) % 3
            gi = t % 2
            qT = qTP[gi]
            X, XK = (xt, "xt") if gi == 0 else (xtB, "xtB")
            blks = ([0] if t > 0 else []) + [1]
            i = 0
            for blk in blks:
                ksrc, kk = (kT[par], "kT%d" % par) if blk == 1 else (kT[prv], "kT%d" % prv)
                for g in range(2):
                    bank = 2
                    pk = "q1"
                    i += 1
                    for half in range(2):
                        T(lambda e, g=g, half=half, bank=bank, ksrc=ksrc: e.matmul(
                            psQ[:, bank + half, :], lhsT=ksrc[g * 64:(g + 1) * 64, :],
                            rhs=qT[g * 64:(g + 1) * 64, half * 4:(half + 1) * 4, :].rearrange("p a t -> p (a t)"), start=True, stop=True), [kk, "qT%d" % gi], [pk])
                    V(lambda e, g=g, bank=bank, blk=blk: e.tensor_tensor(
                        out=sbuf_s.rearrange("p (h q) -> p h q", q=128),
                        in0=psQ[:, bank:bank + 2, :].rearrange("p a (h q) -> p (a h) q", q=128),
                        in1=bias[blk][:, g * 8:(g + 1) * 8, :], op=ALU.add), [pk, "bias"], ["sbuf_s"])
                    A(lambda e, g=g, blk=blk: e.activation(out=PT[:, blk, g * 8:(g + 1) * 8, :], in_=sbuf_s.rearrange("p (h q) -> p h q", q=128),
                                                          func=AF.Exp), ["sbuf_s"], ["PT"])
            for h in range(16):
                g = h // 8
                oreg = psO[:, h // 8, (h % 8) * 64:(h % 8) * 64 + 64]
                for bi_, blk in enumerate(blks):
                    vsrc, vk = (Vb[par], "Vb%d" % par) if blk == 1 else (Vb[prv], "Vb%d" % prv)
                    T(lambda e, h=h, g=g, blk=blk, bi_=bi_, vsrc=vsrc, oreg=oreg: e.matmul(
                        oreg, lhsT=PT[:, blk, h, :], rhs=vsrc[:, g * 64:(g + 1) * 64], start=(bi_ == 0), stop=(bi_ == len(blks) - 1)),
                      ["PT", vk], ["o"])
                for bi_, blk in enumerate(blks):
                    T(lambda e, h=h, blk=blk, bi_=bi_: e.matmul(psD[:, 256 + h:257 + h], lhsT=PT[:, blk, h, :], rhs=ones_b[:, 0:1],
                                                               start=(bi_ == 0), stop=(bi_ == len(blks) - 1)), ["PT", "ones_b"], ["d"])
            attn_finalize(128, gi)
            kind = 1 if t == 0 else 0
            for g in range(4):
                T(lambda e, g=g: e.matmul(psQ[:, 2, g * 128:(g + 1) * 128], lhsT=pin[par][:, g * 128:(g + 1) * 128], rhs=Mp[:, kind, g, :],
                                          start=True, stop=(t == 0)), ["pin%d" % par, "Mp"], ["q1"])
                if t > 0:
                    T(lambda e, g=g: e.matmul(psQ[:, 2, g * 128:(g + 1) * 128], lhsT=pin[prv][:, g * 128:(g + 1) * 128], rhs=Mp[:, 2, g, :],
                                              start=False, stop=True), ["pin%d" % prv, "Mp"], ["q1"])
            A(lambda e: e.copy(out=rTb.rearrange("p g t -> p (g t)"), in_=psQ[:, 2, :]), ["q1"], ["rTb"])
            back(128, x1s[t * 128:(t + 1) * 128, :], X, XK, gi)

        def record(fn, *a):
            rec[0] = []
            fn(*a)
            raw, rec[0] = rec[0], None
            out, run = [], []

            def flush():
                if run:
                    grp = list(run)
                    out.append(lambda: [g() for g in grp])
                    run.clear()
            for eng, th in raw:
                if eng == "pe":
                    run.append(th)
                else:
                    flush()
                    out.append(th)
            flush()
            return out

        np1 = ntiles if stop >= 2 else 0
        if np1:
            for f_ in record(prompt_front, 0):
                f_()
        for t in range(np1):
            bk = record(prompt_back, t)
            fr = record(prompt_front, t + 1) if t + 1 < np1 else []
            i = j = 0
            while i < len(bk) or j < len(fr):
                if j < len(fr) and (i >= len(bk) or j * len(bk) <= i * len(fr)):
                    fr[j]()
                    j += 1
                else:
                    bk[i]()
                    i += 1

        def sample_tile():
            n = NS
            r = slice(0, n)
            D(M1[r], modS[1:17, 0:3, :], ["modS"], ["M1"])
            D(xt[r, :], xs, [], ["xt"])
            front(n, 0, 0)
            if sub < 2:
                return
            kall_f = stage[0].rearrange("p (b c) -> p b c", c=128)
            vall_f = stage[1].rearrange("p (b c) -> p b c", c=128)
            import os as _os
            KD = _os.environ.get("KDIS", "")
            if "a" not in KD: D(kall_f[0:112, :, :], ck[:, 1:113, :].rearrange("b k c -> k b c"), [], ["stage0"])
            if "b" not in KD: D(kall_f[112:127, :, :], ck[:, 113:128, :].rearrange("b k c -> k b c"), [], ["stage0c"])
            if "c" not in KD:
                A(lambda e: e.copy(out=F2[r, 0:128], in_=knf[r, :]), ["knf"], ["F2"])
                D(krow[1], F2[r, 0:128], ["F2"], ["krow0"])
                if "C" not in KD: D(kall_f[127:128, :, :], krow[1], ["krow0", "stage0"], ["stage0b"])
            if "d" not in KD: D(vall_f[0:112, :, :], cv[:, 1:113, :].rearrange("b k c -> k b c"), [], ["stage1"])
            if "e" not in KD: D(vall_f[112:127, :, :], cv[:, 113:128, :].rearrange("b k c -> k b c"), [], ["stage1c"])
            if "f" not in KD:
                D(krow[0], vf[r, :], ["vf"] + (["knf"] if "X" in KD else []), ["krow1"])
                if "F" not in KD: D(vall_f[127:128, :, :], krow[0], ["krow1", "stage1"], ["stage1b"])
            if "g" not in KD: D(kws.rearrange("b k c -> k b c"), kall_f, ["stage0", "stage0b", "stage0c"], ["o_kws"])
            if "h" not in KD: D(vws.rearrange("b k c -> k b c"), vall_f, ["stage1", "stage1b", "stage1c"], ["o_vws"])
            out_keys.extend(["o_kws", "o_vws"])
            if sub < 3:
                return
            A(lambda e: e.copy(out=kall_b, in_=kall_f), ["stage0", "stage0b", "stage0c"], ["kall_b", "bias"])
            V(lambda e: e.tensor_copy(out=vall_b, in_=vall_f), ["stage1", "stage1b", "stage1c"], ["vall_b", "bias"])
            KSS = _os.environ.get("KSS", "z")
            if KSS == "a":
                return
            for rd in range(2):
                for bb in range(8):
                    b = rd * 8 + bb
                    T(lambda e, b=b, bb=bb: e.transpose(out=psT[:, bb, :], in_=kall_b[:, b, :], identity=ident_b), ["kall_b", "ident_b"], ["t"])
                A(lambda e, rd=rd: e.copy(out=kTs[:, rd * 8:(rd + 1) * 8, :], in_=psT[:, :, :]), ["t"], ["kTs", "Mp"])
            if KSS == "b":
                return
            A(lambda e: e.copy(out=qTs, in_=qT[:, :, 0:n].rearrange("p hh b -> p b hh")), ["qT0"], ["qTs"])
            for b in range(n):
                for g in range(2):
                    T(lambda e, b=b, g=g: e.matmul(psD[:, b * 16 + g * 8:b * 16 + g * 8 + 8], lhsT=kTs[g * 64:(g + 1) * 64, b, :],
                                                   rhs=qTs[g * 64:(g + 1) * 64, b, :], start=True, stop=True), ["kTs", "qTs"], ["d"])
            if KSS == "c":
                return
            sbs = sbuf_s[:, 0:256].rearrange("p (b h) -> p b h", h=16)
            V(lambda e: e.tensor_tensor(out=sbs, in0=psD[:, 0:256].rearrange("p (b h) -> p b h", h=16),
                                        in1=bias[1][:, :, 127].unsqueeze(1).to_broadcast([128, 16, 16]), op=ALU.add), ["d", "bias"], ["sbuf_s"])
            if sub < 4:
                return
            PTf = PT.rearrange("p a h q -> p (a h q)")
            G(lambda e: e.memset(PTf, 0.0), [], ["PT"])
            diag = bass.AP(tensor=PTf.tensor, offset=PTf.offset, ap=[list(PTf.ap[0]), [257, 16], [16, 16]])
            A(lambda e: e.activation(out=diag, in_=sbs, func=AF.Exp), ["sbuf_s", "PT"], ["PT"])
            if sub < 5:
                return
            PTp = PTf.rearrange("p (b h c) -> p b h c", h=16, c=16)
            for h in range(16):
                g = h // 8
                oreg = psO[r, h // 8, (h % 8) * 64:(h % 8) * 64 + 64]
                for b in range(n):
                    T(lambda e, h=h, g=g, b=b, oreg=oreg: e.matmul(oreg, lhsT=PTp[:, b, h, :], rhs=vall_b[:, b, g * 64:(g + 1) * 64],
                                                                   start=(b == 0), stop=(b == n - 1)), ["PT", "vall_b"], ["o"])
                for b in range(n):
                    T(lambda e, h=h, b=b: e.matmul(psD[r, 256 + h:257 + h], lhsT=PTp[:, b, h, :], rhs=ones_b[:, 0:1],
                                                   start=(b == 0), stop=(b == n - 1)), ["PT", "ones_b"], ["d"])
            attn_finalize(n, 0)
            if sub < 6:
                return
            pin_s = pin[0]
            tot = F1
            hoff = 0
            stage_all = a1t[:, 0:4096]
            stk = ["stage0", "stage0b", "stage0c", "stage1", "stage1b", "stage1c"]
            for g, w in enumerate(POOL_W):
                nh = w - 1
                hist = stage_all[r, hoff:hoff + nh * 128].rearrange("p (r c) -> p r c", c=128)
                hoff += nh * 128
                cs_ = slice(g * 128, (g + 1) * 128)
                D(hist, spool[:, 15 - nh:15, g * 128:(g + 1) * 128], [], ["hist%d" % g] + stk)
                V(lambda e, hist=hist, cs_=cs_: e.tensor_reduce(out=tot[r, cs_], in_=hist.rearrange("p r c -> p c r"), axis=AX.X, op=ALU.add),
                  ["hist%d" % g], ["F1"])
                V(lambda e, cs_=cs_: e.tensor_tensor(out=tot[r, cs_], in0=tot[r, cs_], in1=pin_s[r, cs_], op=ALU.add), ["F1", "pin0"], ["F1"])
                V(lambda e, cs_=cs_, w=w: e.scalar_tensor_tensor(out=tot[r, cs_], in0=tot[r, cs_], scalar=1.0 / w, in1=pin_s[r, cs_],
                                                                 op0=ALU.mult, op1=ALU.subtract), ["F1", "pin0"], ["F1"])
                T(lambda e, g=g, cs_=cs_: e.transpose(out=psQ[:, 0, g * 128:g * 128 + n], in_=tot[r, cs_], identity=ident_f[0:n, 0:n]),
                  ["F1", "ident_f"], ["q0"])
            A(lambda e: e.copy(out=rTb[:, :, 0:n], in_=psQ[:, 0, :].rearrange("p (g t) -> p g t", t=128)[:, :, 0:n]), ["q0"], ["rTb"])
            D(pools[:, 0:14, :], spool[:, 1:15, :], [], ["o_pools"])
            D(pools[:, 14, :], pin_s[r, :], ["pin0"], ["o_pools2"])
            out_keys.extend(["o_pools", "o_pools2"])
            if sub < 7:
                return
            back(n, x1s[NT * 128:NT * 128 + n, :], xt, "xt", 0)
            if debug:
                dump("atts", att[r, :], [n, 1024], ["att"])
                dump("x1s", F3[r, :], [n, 1024], ["F3"])

        if stop >= 3:
            sample_tile()
            if ntiles == NT:
                out_keys.extend(["o_kwp", "o_vwp", "o_poolp"])

        P.barrier()
        A1.off = A1.mark
        M2 = A1.alloc([3, 1024])
        wq_b = A1.alloc([8, 2048], BF16)
        skT = A1.alloc([16, 128])
        qfp = A1.alloc([2048]); qT32 = A1.alloc([16, 128]); s_sb = A1.alloc([16, 128])
        mx = A1.alloc([16, 16]); mi = A1.alloc([16, 16], U32); mif = A1.alloc([16, 16])
        wk = A1.alloc([256]); cs = A1.alloc([8, 256])
        bs = A1.alloc([8, 16]); bp = A1.alloc([8, 16], U32); au = A1.alloc([8, 16], U32); bu = A1.alloc([8, 16], U32)
        af = A1.alloc([8, 16]); bf = A1.alloc([8, 16])
        oh = A1.alloc([4, 16, 16]); i0s = A1.alloc([8, 16]); i1s = A1.alloc([8, 16])
        idx = A1.alloc([128], I32); ge = A1.alloc([8, 16]); gate = A1.alloc([128])
        acc = A1.alloc([1024])
        junkb = A1.alloc([1024], BF16)
        idxT = A1.alloc([128], I32); gateT = A1.alloc([128])
        preT = A1.alloc([128]); gT = A1.alloc([128]); coefT = A1.alloc([128])
        dgb = [A1.alloc([128], BF16) for _ in range(2)]
        gs = [A1.alloc([2048], BF16) for _ in range(NSLOT)]

        for kc in range(8):
            P.dma("pool", lambda e, kc=kc: e.dma_start(out=wq_b[:, kc, :], in_=w_query[kc * 128:(kc + 1) * 128, :]), [], ["wts"])
        for j in range(16):
            sg = stage[j % 2]
            sk = "stage%d" % (j % 2)
            D(sg[:, 0:128], sub_keys[j], [], [sk])
            T(lambda e, j=j, sg=sg: e.transpose(out=psO[:, j % 2, 0:128], in_=sg[:, 0:128], identity=ident_f), [sk, "ident_f"], ["o%d" % (j % 2)])
            A(lambda e, j=j: e.copy(out=skT[:, j, :], in_=psO[:, j % 2, 0:128]), ["o%d" % (j % 2)], ["skT"])
        D(M2, bass.AP(tensor=modS.tensor, offset=modS[0:1, 3:6, :].offset, ap=[[0, 128], [1024, 3], [1, 1024]]), ["modS"], ["M2a", "M2g"])

        slot_i = [0]

        def top16(n, src, srck, width, vals, idxs, outk):
            r = slice(0, n)
            V(lambda e: e.max(out=vals[r, 0:8], in_=src), [srck], [outk + "a"])
            V(lambda e: e.max_index(out=idxs[r, 0:8], in_max=vals[r, 0:8], in_values=src), [srck, outk + "a"], [outk + "b"])
            V(lambda e: e.match_replace(out=wk[r, 0:width], in_to_replace=vals[r, 0:8], in_values=src, imm_value=-1e30), [srck, outk + "a"], ["wk"])
            V(lambda e: e.max(out=vals[r, 8:16], in_=wk[r, 0:width]), ["wk"], [outk + "c"])
            V(lambda e: e.max_index(out=idxs[r, 8:16], in_max=vals[r, 8:16], in_values=wk[r, 0:width]), ["wk", outk + "c"], [outk + "d"])

        xtp = [xt, A1.alloc([1024])]
        hbp = [hb, A1.alloc([1024], BF16)]
        idxTp = [idxT, A1.alloc([128], I32)]
        gateTp = [gateT, A1.alloc([128])]

        def stage1_steps(t, n, par, xsrc=None, pre=()):
            r = slice(0, n)
            X, XK, HB, HK = xtp[par], "xt%d" % par, hbp[par], "hb%d" % par
            IT, ITK, GT, GTK = idxTp[par], "idxT%d" % par, gateTp[par], "gateT%d" % par
            st = list(pre)
            ss, ms, rs = st_small[r, 0:1], st_small[r, 1:2], st_small[r, 2:3]
            st.append(lambda: D(X[r, :], x1s[t * 128:t * 128 + n, :] if xsrc is None else xsrc, ["x1s"], [XK]))
            st.append(lambda: A(lambda e: e.activation(out=F1[r, :], in_=X[r, :], func=AF.Square, accum_out=ss), [XK], ["F1", "ss"]))
            st.append(lambda: V(lambda e: e.tensor_scalar(out=ms, in0=ss, scalar1=1.0 / 1024, scalar2=EPS, op0=ALU.mult, op1=ALU.add), ["ss"], ["ms"]))
            st.append(lambda: A(lambda e: e.activation(out=ms, in_=ms, func=AF.Sqrt), ["ms"], ["ms"]))
            st.append(lambda: V(lambda e: e.reciprocal(out=rs, in_=ms), ["ms"], ["rs"]))
            st.append(lambda: V(lambda e: e.scalar_tensor_tensor(out=F2[r, :], in0=X[r, :], scalar=rs, in1=M2[r, 1, :], op0=ALU.mult, op1=ALU.mult),
                                [XK, "rs", "M2a"], ["F2"]))
            st.append(lambda: V(lambda e: e.tensor_tensor(out=HB[r, :], in0=F2[r, :], in1=M2[r, 0, :], op=ALU.add), ["F2", "M2a"], [HK]))
            st.append(lambda: transpose8(n, HB, HK, hT, "hT"))
            for c4 in range(4):
                def f(c4=c4):
                    for kc in range(8):
                        T(lambda e, kc=kc: e.matmul(psD[r, :], lhsT=hT[:, kc, 0:n], rhs=wq_b[:, kc, c4 * 512:(c4 + 1) * 512],
                                                    start=(kc == 0), stop=(kc == 7)), ["hT", "wts"], ["d"])
                    A(lambda e: e.copy(out=qfp[r, c4 * 512:(c4 + 1) * 512], in_=psD[r, :]), ["d"], ["qfp"])
                st.append(f)
            for r4 in range(4):
                def f(r4=r4):
                    for jj in range(4):
                        j = r4 * 4 + jj
                        T(lambda e, j=j, jj=jj: e.transpose(out=psD[:, jj * 128:jj * 128 + n], in_=qfp[r, j * 128:(j + 1) * 128],
                                                            identity=ident_f[0:n, 0:n]), ["qfp", "ident_f"], ["d"])
                    A(lambda e: e.copy(out=qT32[:, r4 * 4:(r4 + 1) * 4, 0:n], in_=psD[:, :].rearrange("p (j t) -> p j t", t=128)[:, :, 0:n]),
                      ["d"], ["qT32"])
                st.append(f)
            for r4 in range(4):
                def f(r4=r4):
                    for jj in range(4):
                        j = r4 * 4 + jj
                        T(lambda e, j=j, jj=jj: e.matmul(psD[r, jj * 128:(jj + 1) * 128], lhsT=qT32[:, j, 0:n], rhs=skT[:, j, :],
                                                         start=True, stop=True), ["qT32", "skT"], ["d"])
                    A(lambda e: e.copy(out=s_sb[r, r4 * 4:(r4 + 1) * 4, :].rearrange("p j k -> p (j k)"), in_=psD[r, :]), ["d"], [("s_sb", r4)])
                st.append(f)
            for j in range(16):
                st.append(lambda j=j: top16(n, s_sb[r, j, :], ("s_sb", j // 4), 128, mx[:, j, :], mi[:, j, :], "tk%d" % j))
            tka = ["tk%d%s" % (j, c) for j in range(16) for c in "ac"]
            tkb = ["tk%d%s" % (j, c) for j in range(16) for c in "bd"]
            mxv = mx[r].rearrange("p (h two) k -> p h two k", two=2)
            st.append(lambda: V(lambda e: e.tensor_tensor(out=cs[r].rearrange("p h (a b) -> p h a b", b=16),
                                                          in0=mxv[:, :, 0, :].unsqueeze(3).to_broadcast([n, 8, 16, 16]),
                                                          in1=mxv[:, :, 1, :].unsqueeze(2).to_broadcast([n, 8, 16, 16]), op=ALU.add), tka, ["cs"]))
            for h in range(8):
                st.append(lambda h=h: top16(n, cs[r, h, :], "cs", 256, bs[:, h, :], bp[:, h, :], "ck%d" % h))
            cka = ["ck%d%s" % (h, c) for h in range(8) for c in "ac"]
            ckb = ["ck%d%s" % (h, c) for h in range(8) for c in "bd"]
            st.append(lambda: V(lambda e: e.tensor_single_scalar(out=au[r], in_=bp[r], scalar=4, op=ALU.logical_shift_right), ckb, ["au"]))
            st.append(lambda: V(lambda e: e.tensor_single_scalar(out=bu[r], in_=bp[r], scalar=15, op=ALU.bitwise_and), ckb, ["bu"]))
            st.append(lambda: V(lambda e: e.tensor_copy(out=af[r], in_=au[r]), ["au"], ["af"]))
            st.append(lambda: V(lambda e: e.tensor_copy(out=bf[r], in_=bu[r]), ["bu"], ["bf"]))
            st.append(lambda: V(lambda e: e.tensor_copy(out=mif[r], in_=mi[r]), tkb, ["mif"]))
            mifv = mif[r].rearrange("p (h two) k -> p h two k", two=2)
            for (sel_, selk, which, dst, dstk) in ((af, "af", 0, i0s, "i0s"), (bf, "bf", 1, i1s, "i1s")):
                for hq in range(2):
                    hs = slice(hq * 4, hq * 4 + 4)
                    st.append(lambda sel_=sel_, selk=selk, hs=hs: V(lambda e: e.tensor_tensor(
                        out=oh[r], in0=sel_[r, hs, :].unsqueeze(3).to_broadcast([n, 4, 16, 16]),
                        in1=iota16[r, :].unsqueeze(1).unsqueeze(1).to_broadcast([n, 4, 16, 16]), op=ALU.is_equal), [selk, "iota16", "oh"], ["oh"]))
                    st.append(lambda hs=hs, which=which: V(lambda e: e.tensor_tensor(
                        out=oh[r], in0=oh[r], in1=mifv[:, hs, which, :].unsqueeze(2).to_broadcast([n, 4, 16, 16]), op=ALU.mult), ["oh", "mif"], ["oh"]))
                    st.append(lambda dst=dst, dstk=dstk, hs=hs: V(lambda e: e.tensor_reduce(out=dst[r, hs, :], in_=oh[r], axis=AX.X, op=ALU.add),
                                                                 ["oh"], [dstk]))
            st.append(lambda: V(lambda e: e.scalar_tensor_tensor(out=IT[r, :], in0=i0s[r].rearrange("p h k -> p (h k)"), scalar=128.0,
                                                                 in1=i1s[r].rearrange("p h k -> p (h k)"), op0=ALU.mult, op1=ALU.add),
                                ["i0s", "i1s"], [ITK]))
            zz = st_small[r, 48:56]
            st.append(lambda: V(lambda e: e.tensor_tensor(out=ge[r], in0=bs[r], in1=bs[r, :, 0:1].to_broadcast([n, 8, 16]), op=ALU.subtract), cka, ["ge"]))
            st.append(lambda: A(lambda e: e.activation(out=ge[r], in_=ge[r], func=AF.Exp), ["ge"], ["ge"]))
            st.append(lambda: V(lambda e: e.tensor_reduce(out=zz, in_=ge[r], axis=AX.X, op=ALU.add), ["ge"], ["zz"]))
            st.append(lambda: V(lambda e: e.reciprocal(out=zz, in_=zz), ["zz"], ["zz"]))
            st.append(lambda: V(lambda e: e.tensor_tensor(out=GT[r, :].rearrange("p (h k) -> p h k", k=16), in0=ge[r],
                                                          in1=zz.unsqueeze(2).to_broadcast([n, 8, 16]), op=ALU.mult), ["ge", "zz"], [GTK]))
            if debug and t == 0:
                st.append(lambda: dump("idx0", IT, [128, 128], [ITK], I32))
                st.append(lambda: dump("gate0", GT, [128, 128], [GTK]))
            rec[0] = []
            for f_ in st:
                f_()
            flat, rec[0] = [th for _, th in rec[0]], None
            return flat

        def stage2(t, n, par, y_dst, okey, filler):
            r = slice(0, n)
            X, XK, HB, HK = xtp[par], "xt%d" % par, hbp[par], "hb%d" % par
            IT, ITK, GT, GTK = idxTp[par], "idxT%d" % par, gateTp[par], "gateT%d" % par
            slots = {}

            def stA(ex):
                sl = slot_i[0] % NSLOT
                slot_i[0] += 1
                slots[ex] = sl
                P.dma("pool", lambda e: e.indirect_dma_start(
                    out=gs[sl][r, :], out_offset=None, in_=uvb, in_offset=bass.IndirectOffsetOnAxis(ap=IT[r, ex:ex + 1], axis=0)),
                    [ITK] + tabs_keys, ["gs%d" % sl])

            def stB(ex):
                sl = slots[ex]
                V(lambda e: e.scalar_tensor_tensor(
                    out=junkb[r, :], in0=gs[sl][r, 0:1024], scalar=1.0, in1=HB[r, :],
                    op0=ALU.mult, op1=ALU.mult, accum_out=preT[r, ex:ex + 1]), ["gs%d" % sl, HK], [("preT", ex), "junkb"])

            def stC1(ex):
                A(lambda e: e.activation(out=gT[r, ex:ex + 1], in_=preT[r, ex:ex + 1], func=AF.Gelu), [("preT", ex)], [("gT", ex)])
                A(lambda e: e.activation(out=coefT[r, ex:ex + 1], in_=gT[r, ex:ex + 1], func=AF.Copy, scale=GT[r, ex:ex + 1]),
                  [("gT", ex), GTK], [("coefT", ex)])

            def stC2(ex):
                sl = slots[ex]
                b2 = ex % 2
                A(lambda e: e.activation(out=dgb[b2][r, 0:n], in_=ident_b[r, 0:n], func=AF.Copy, scale=coefT[r, ex:ex + 1]),
                  [("coefT", ex), "ident_b"], ["dg%d" % b2])
                for half in range(2):
                    T(lambda e, half=half: e.matmul(
                        psO[r, half, :], lhsT=dgb[b2][r, 0:n], rhs=gs[sl][r, 1024 + half * 512:1024 + (half + 1) * 512],
                        start=(ex == 0), stop=(ex == 127)), ["dg%d" % b2, "gs%d" % sl], ["o0", "o1"])

            fi = 0
            per = (len(filler) + 117) // 118 if filler else 0
            for i in range(128 + 4):
                if i < 128:
                    stA(i)
                if 0 <= i - 1 < 128:
                    stB(i - 1)
                if 0 <= i - 2 < 128:
                    stC1(i - 2)
                if 0 <= i - 4 < 128:
                    stC2(i - 4)
                if filler and i >= 4:
                    target = min(len(filler), -(-(i - 3) * len(filler) // (FILL_END - 3)))
                    while fi < target:
                        filler[fi]()
                        fi += 1
            while filler and fi < len(filler):
                filler[fi]()
                fi += 1
            V(lambda e: e.tensor_tensor(out=acc[r, :], in0=psO[r, :, :].rearrange("p a c -> p (a c)"), in1=M2[r, 2, :], op=ALU.mult),
              ["o0", "o1", "M2g"], ["acc"])
            V(lambda e: e.tensor_tensor(out=acc[r, :], in0=acc[r, :], in1=X[r, :], op=ALU.add), ["acc", XK], ["acc"])
            D(y_dst, acc[r, :], ["acc"], [okey])
            out_keys.append(okey)

        maskg = A0.alloc([8]); esel_b = A0.alloc([16], BF16)
        D(maskg, c_maskg, [], ["maskg"])
        D(st_small[:, 8:24], c_esel, [], ["hr"])
        V(lambda e: e.tensor_copy(out=esel_b, in_=st_small[:, 8:24]), ["hr"], ["esel_b"])
        its = A1.alloc([16], I32); gts = A1.alloc([16]); selt = A1.alloc([128]); selr = A1.alloc([16])

        def stage2_sample(par, y_dst, okey):
            X, XK, HB, HK = xtp[par], "xt%d" % par, hbp[par], "hb%d" % par
            IT, ITK, GT, GTK = idxTp[par], "idxT%d" % par, gateTp[par], "gateT%d" % par
            for (src, srck, dst, dstk) in ((IT, ITK, its, "its"), (GT, GTK, gts, "gts")):
                V(lambda e, src=src: e.tensor_copy(out=selt, in_=src), [srck], ["selt"])
                V(lambda e: e.tensor_tensor(out=selt.rearrange("p (g j) -> p g j", j=16), in0=selt.rearrange("p (g j) -> p g j", j=16),
                                            in1=maskg.unsqueeze(2).to_broadcast([128, 8, 16]), op=ALU.mult), ["selt", "maskg"], ["selt"])
                V(lambda e: e.tensor_reduce(out=selr, in_=selt.rearrange("p (g j) -> p j g", j=16), axis=AX.X, op=ALU.add),
                  ["selt"], ["selr"])
                V(lambda e, dst=dst: e.tensor_copy(out=dst, in_=selr), ["selr"], [dstk])
            slots = {}
            for j in range(16 + 4):
                if j < 16:
                    sl = slot_i[0] % NSLOT
                    slot_i[0] += 1
                    slots[j] = sl
                    P.dma("pool", lambda e, sl=sl, j=j: e.indirect_dma_start(
                        out=gs[sl][:, :], out_offset=None, in_=uvb, in_offset=bass.IndirectOffsetOnAxis(ap=its[:, j:j + 1], axis=0)),
                        ["its"] + tabs_keys, ["gs%d" % sl])
                if 0 <= j - 1 < 16:
                    ex = j - 1
                    sl = slots[ex]
                    V(lambda e, sl=sl, ex=ex: e.scalar_tensor_tensor(
                        out=junkb[:, :], in0=gs[sl][:, 0:1024], scalar=1.0, in1=HB[:, :],
                        op0=ALU.mult, op1=ALU.mult, accum_out=preT[:, ex:ex + 1]), ["gs%d" % sl, HK], [("preT", ex), "junkb"])
                if 0 <= j - 2 < 16:
                    ex = j - 2
                    A(lambda e, ex=ex: e.activation(out=gT[:, ex:ex + 1], in_=preT[:, ex:ex + 1], func=AF.Gelu), [("preT", ex)], [("gT", ex)])
                    A(lambda e, ex=ex: e.activation(out=coefT[:, ex:ex + 1], in_=gT[:, ex:ex + 1], func=AF.Copy, scale=gts[:, ex:ex + 1]),
                      [("gT", ex), "gts"], [("coefT", ex)])
                if 0 <= j - 4 < 16:
                    ex = j - 4
                    sl = slots[ex]
                    b2 = ex % 2
                    A(lambda e, ex=ex, b2=b2: e.activation(out=dgb[b2][:, 0:NS], in_=esel_b, func=AF.Copy, scale=coefT[:, ex:ex + 1]),
                      [("coefT", ex), "esel_b"], ["dg%d" % b2])
                    for half in range(2):
                        T(lambda e, half=half, sl=sl, b2=b2, ex=ex: e.matmul(
                            psO[0:NS, half, :], lhsT=dgb[b2][:, 0:NS], rhs=gs[sl][:, 1024 + half * 512:1024 + (half + 1) * 512],
                            start=(ex == 0), stop=(ex == 15)), ["dg%d" % b2, "gs%d" % sl], ["o0", "o1"])
            r = slice(0, NS)
            V(lambda e: e.tensor_tensor(out=acc[r, :], in0=psO[r, :, :].rearrange("p a c -> p (a c)"), in1=M2[r, 2, :], op=ALU.mult),
              ["o0", "o1", "M2g"], ["acc"])
            V(lambda e: e.tensor_tensor(out=acc[r, :], in0=acc[r, :], in1=X[r, :], op=ALU.add), ["acc", XK], ["acc"])
            D(y_dst, acc[r, :], ["acc"], [okey])
            out_keys.append(okey)

        npt = ntiles if stop >= 4 else 0
        if npt:
            for f_ in stage1_steps(0, 128, 0):
                f_()
        xrep = bass.AP(tensor=x1s.tensor, offset=x1s[NT * 128:NT * 128 + 1, :].offset, ap=[[0, 8], [1024, NS], [1, 1024]])
        m2a_rep = bass.AP(tensor=modS.tensor, offset=modS[1:2, 3:5, :].offset, ap=[[0, 8], [6 * 1024, NS], [1024, 2], [1, 1024]])
        m2g_rep = bass.AP(tensor=modS.tensor, offset=modS[1:2, 5:6, :].offset, ap=[[0, 8], [6 * 1024, NS], [1, 1024]])
        s_pre = [lambda: D(M2[:, 0:2, :], m2a_rep, ["modS"], ["M2a"])]
        do_s = stop >= 5 and npt == NT
        for t in range(npt):
            if t + 1 < npt:
                nxt = stage1_steps(t + 1, 128, (t + 1) % 2)
            elif do_s:
                nxt = stage1_steps(NT, 128, NT % 2, xsrc=xrep, pre=s_pre)
            else:
                nxt = None
            stage2(t, 128, t % 2, yp[t * 128:(t + 1) * 128, :], "o_yp%d" % t, nxt)
        if stop >= 5:
            par = NT % 2
            if not do_s:
                for f_ in stage1_steps(NT, 128, par, xsrc=xrep, pre=s_pre):
                    f_()
            D(M2[:, 2, :], m2g_rep, ["modS"], ["M2g"])
            stage2_sample(par, ys, "o_ys")

        P.op("sp", lambda e: e.nop(), out_keys, [])
        P.emit()
    return nc


_NC_CACHE = {}


def _in_maps(inp):
    ident, R, Mp, iota, maskg, esel = _consts()
    f = lambda a: np.ascontiguousarray(np.asarray(a, dtype=np.float32))
    shared = {
        "table": f(inp["rel_bias_table"]), "ada_w": f(inp["ada_w"][0]), "ada_b": f(inp["ada_b"]), "g1": f(inp["norm1_g"]),
        "w_in": f(inp["w_in"][0]), "gq": f(inp["q_norm_g"]), "gk": f(inp["k_norm_g"]), "sinks": f(inp["attn_sinks"]),
        "pool_w": f(inp["pool_w"][0]), "pool_scale": f(inp["pool_scale"]), "w_out": f(inp["w_out"][0]), "g2": f(inp["norm2_g"]),
        "w_query": f(inp["peer_w_query"][0]), "sub_keys": f(inp["peer_sub_keys"][0]).reshape(16, 128, 128),
        "peer_u": f(inp["peer_u"][0]), "peer_v": f(inp["peer_v"][0]),
        "c_ident": ident, "c_R": R, "c_Mp": Mp, "c_iota": iota, "c_maskg": maskg, "c_esel": esel,
    }
    maps = []
    for c in range(8):
        sl = slice(c * NS, (c + 1) * NS)
        m = dict(shared)
        m["xp"] = f(inp["x_prompt"][c])
        m["xs"] = f(inp["x_sample"][sl, 0, :])
        m["ck"] = f(inp["cache_k_win"][0, sl]).reshape(NS, 128, 128)
        m["cv"] = f(inp["cache_v_win"][0, sl]).reshape(NS, 128, 128)
        m["spool"] = f(inp["state_pool"][0, sl])
        m["c17"] = np.concatenate([f(inp["c_prompt"][c:c + 1]), f(inp["c_sample"][sl])], axis=0)
        maps.append(m)
    return maps


def kernel(**inputs):
    if "nc" not in _NC_CACHE:
        _NC_CACHE["nc"] = build_nc(False)
    nc = _NC_CACHE["nc"]
    res = run_bass_kernel_spmd(nc, _in_maps(inputs), core_ids=list(range(8)))
    R = res.results
    y_prompt = np.stack([R[c]["yp"].reshape(2048, 1024) for c in range(8)], 0)
    y_sample = np.concatenate([R[c]["ys"] for c in range(8)], 0).reshape(128, 1, 1024)
    kwp = np.stack([R[c]["kwp"].reshape(128, 2, 64) for c in range(8)], 0)[None]
    vwp = np.stack([R[c]["vwp"].reshape(128, 2, 64) for c in range(8)], 0)[None]
    poolp = np.stack([R[c]["poolp"] for c in range(8)], 0)[None]
    kws = np.concatenate([R[c]["kws"].reshape(NS, 128, 2, 64) for c in range(8)], 0)[None]
    vws = np.concatenate([R[c]["vws"].reshape(NS, 128, 2, 64) for c in range(8)], 0)[None]
    pools = np.concatenate([R[c]["pools"] for c in range(8)], 0)[None]
    return tuple(np.ascontiguousarray(a, dtype=np.float32) for a in (y_prompt, y_sample, kwp, vwp, poolp, kws, vws, pools))
```
